# Optimizing a Trainium2 kernel written in Bass

```python
import jax, jax.numpy as jnp
from jax import lax
import numpy as np

D_MODEL = 2048
BATCH = 16
SEQ = 2048
DEPTH = 1
DEC_BATCH = 32
DEC_SEQ = 4
PAST_LEN = 16384
PAGE_SIZE = 128

HEAD_DIM = 128
HEADS_PER_GROUP = 4
GROUPS = ((128, 1), (512, 4), (2048, 16))
N_GROUPS = 3
N_ATT_HEADS = N_GROUPS * HEADS_PER_GROUP
ATT_WIDTH = N_ATT_HEADS * HEAD_DIM
ATT_OUT = HEADS_PER_GROUP * HEAD_DIM
BAND = 128
ATT_SCALE = HEAD_DIM ** -0.5
Q_BLOCK = 128
D_CONV = D_MODEL
CONV_W = 3
D_FF = 5632
LN_EPS = 1e-5
ALPHA = (2.0 * DEPTH) ** 0.25
BETA = (8.0 * DEPTH) ** -0.25

kernel_name = 'hybrid_shortconv_dilated_swa_convffn_step'


def _proj_sizes():
    return (D_CONV, D_CONV, D_CONV, ATT_WIDTH, ATT_WIDTH, ATT_WIDTH, D_MODEL, D_MODEL)


def _split_points():
    pts, acc = [], 0
    for s in _proj_sizes()[:-1]:
        acc += s
        pts.append(acc)
    return pts


def _layernorm(x, g, b):
    xf = x.astype(jnp.float32)
    mu = jnp.mean(xf, axis=-1, keepdims=True)
    var = jnp.mean(jnp.square(xf - mu), axis=-1, keepdims=True)
    return ((xf - mu) * lax.rsqrt(var + LN_EPS) * g.astype(jnp.float32) + b.astype(jnp.float32)).astype(x.dtype)


def _causal_dwconv(u, prev, w):
    t = u.shape[1]
    up = jnp.concatenate([prev.astype(u.dtype), u], axis=1)
    y = sum(w[i] * up[:, i:i + t] for i in range(CONV_W))
    return y.astype(u.dtype), up[:, t:]


def _alibi_slopes():
    h = jnp.arange(1, N_ATT_HEADS + 1, dtype=jnp.float32)
    return jnp.exp2(-8.0 * h / N_ATT_HEADS).reshape(N_GROUPS, HEADS_PER_GROUP)


def _dilated_band_prompt(q, k, v, slopes, dil):
    n, s, h, dh = q.shape
    u = s // dil
    nblk = -(-u // Q_BLOCK)
    upad = nblk * Q_BLOCK

    def classes(a, front):
        a = a.reshape(n, u, dil, h, dh).transpose(0, 2, 1, 3, 4)
        return jnp.pad(a, ((0, 0), (0, 0), (front, upad - u), (0, 0), (0, 0)))

    qb = classes(q, 0).reshape(n, dil, nblk, Q_BLOCK, h, dh)

    def windows(a):
        a = classes(a, Q_BLOCK).reshape(n, dil, nblk + 1, Q_BLOCK, h, dh)
        return jnp.concatenate([a[:, :, :-1], a[:, :, 1:]], axis=3)

    kw, vw = windows(k), windows(v)
    qi = jnp.arange(Q_BLOCK)[:, None]
    kj = jnp.arange(2 * Q_BLOCK)[None, :]
    du = Q_BLOCK + qi - kj
    uk = jnp.arange(nblk)[:, None, None] * Q_BLOCK + kj[None] - Q_BLOCK
    valid = (du >= 0) & (du <= BAND) & (uk >= 0)
    bias = slopes[:, None, None] * (dil * du).astype(jnp.float32)[None]
    sc = jnp.einsum('nrbqhd,nrbkhd->nrbhqk', qb, kw).astype(jnp.float32) * ATT_SCALE - bias
    sc = jnp.where(valid[None, None, :, None], sc, -jnp.inf)
    m = jnp.max(sc, axis=-1, keepdims=True)
    p = jnp.exp(sc - m)
    den = jnp.sum(p, axis=-1, keepdims=True)
    o = jnp.einsum('nrbhqk,nrbkhd->nrbqhd', p / den, vw.astype(jnp.float32))
    lse = jnp.swapaxes((m + jnp.log(den))[..., 0], 3, 4)

    def back(a):
        a = a.reshape((n, dil, upad) + a.shape[4:])[:, :, :u]
        a = jnp.swapaxes(a, 1, 2)
        return a.reshape((n, s) + a.shape[3:])

    return back(o), back(lse)


def _dilated_band_sample(q, k_all, v_all, slopes, dil):
    t = q.shape[1]
    l = k_all.shape[1] - t
    j = jnp.arange(BAND + 1)
    idx = l + jnp.arange(t)[:, None] - dil * j[None, :]
    valid = idx >= 0
    idx = jnp.maximum(idx, 0)
    kg = k_all[:, idx]
    vg = v_all[:, idx]
    sc = jnp.einsum('nthd,ntjhd->nhtj', q, kg).astype(jnp.float32) * ATT_SCALE \
        - slopes[:, None, None] * (dil * j).astype(jnp.float32)
    sc = jnp.where(valid, sc, -jnp.inf)
    m = jnp.max(sc, axis=-1, keepdims=True)
    p = jnp.exp(sc - m)
    den = jnp.sum(p, axis=-1, keepdims=True)
    o = jnp.einsum('nhtj,ntjhd->nthd', p / den, vg.astype(jnp.float32))
    lse = jnp.swapaxes((m + jnp.log(den))[..., 0], 1, 2)
    return o, lse


def _layer(x, conv_mix_prev, kv_prev, conv_ffn_prev, w_in, w_conv_mix, w_branch_a, w_branch_b, w_out,
           ln1_g, ln1_b, w_up, w_conv_ffn, w_down, ln2_g, ln2_b):
    n, t, _ = x.shape
    z = jnp.einsum('ntd,de->nte', x, w_in)
    h, gate_b, gate_c, q, k, v, g_a, g_b = jnp.split(z, _split_points(), axis=-1)

    ya, conv_mix_new = _causal_dwconv(gate_c * h, conv_mix_prev, w_conv_mix)
    ya = gate_b * ya

    shp = (n, t, N_GROUPS, HEADS_PER_GROUP, HEAD_DIM)
    q, k, v = q.reshape(shp), k.reshape(shp), v.reshape(shp)
    slopes = _alibi_slopes()
    outs, lses, kv_new = [], [], []
    for g, (win, dil) in enumerate(GROUPS):
        kg, vg = k[:, :, g], v[:, :, g]
        if kv_prev is None:
            o, l = _dilated_band_prompt(q[:, :, g], kg, vg, slopes[g], dil)
        else:
            kg = jnp.concatenate([kv_prev[g][:, :, 0].astype(x.dtype), kg], axis=1)
            vg = jnp.concatenate([kv_prev[g][:, :, 1].astype(x.dtype), vg], axis=1)
            o, l = _dilated_band_sample(q[:, :, g], kg, vg, slopes[g], dil)
        keep = min(win, kg.shape[1])
        kv_new.append(jnp.stack([kg[:, kg.shape[1] - keep:], vg[:, vg.shape[1] - keep:]], axis=2))
        outs.append(o)
        lses.append(l)
    wg = jax.nn.softmax(jnp.stack(lses), axis=0)
    yb = jnp.einsum('gnth,gnthd->nthd', wg, jnp.stack(outs)).reshape(n, t, ATT_OUT).astype(x.dtype)

    mixed = jax.nn.sigmoid(g_a) * (ya @ w_branch_a) + jax.nn.sigmoid(g_b) * (yb @ w_branch_b)
    x1 = _layernorm(ALPHA * x + mixed @ w_out, ln1_g, ln1_b)

    a, b = jnp.split(x1 @ w_up, 2, axis=-1)
    a, conv_ffn_new = _causal_dwconv(a, conv_ffn_prev, w_conv_ffn)
    y = _layernorm(ALPHA * x1 + (jax.nn.silu(a) * b) @ w_down, ln2_g, ln2_b)
    return y, conv_mix_new, kv_new, conv_ffn_new


def setup_inputs(seed: int = 0) -> dict:
    key = jax.random.key(seed)
    ks = jax.random.split(key, 24)
    f32 = jnp.float32

    def nrm(k, shape, scale):
        return jax.random.normal(k, shape, f32) * scale

    col_scale = jnp.concatenate([jnp.full((sz,), sc, f32) for sz, sc in
                                 zip(_proj_sizes(), (BETA, 1.0, 1.0, 1.0, 1.0, BETA, 1.0, 1.0))])
    w_in = nrm(ks[0], (D_MODEL, sum(_proj_sizes())), D_MODEL ** -0.5) * col_scale
    kv_shape = lambda win: (DEC_BATCH, min(win, PAST_LEN), 2, HEADS_PER_GROUP, HEAD_DIM)
    return {
        'x_prompt': nrm(ks[1], (BATCH, SEQ, D_MODEL), 1.0),
        'x_sample': nrm(ks[2], (DEC_BATCH, DEC_SEQ, D_MODEL), 1.0),
        'state_conv_mix': nrm(ks[3], (DEC_BATCH, CONV_W - 1, D_CONV), 1.0),
        'cache_kv0': nrm(ks[4], kv_shape(GROUPS[0][0]), 1.0),
        'cache_kv1': nrm(ks[5], kv_shape(GROUPS[1][0]), 1.0),
        'cache_kv2': nrm(ks[6], kv_shape(GROUPS[2][0]), 1.0),
        'state_conv_ffn': nrm(ks[7], (DEC_BATCH, CONV_W - 1, D_FF), 1.0),
        'w_in': w_in,
        'w_conv_mix': nrm(ks[8], (CONV_W, D_CONV), CONV_W ** -0.5),
        'w_branch_a': nrm(ks[9], (D_CONV, D_MODEL), BETA * D_CONV ** -0.5),
        'w_branch_b': nrm(ks[10], (ATT_OUT, D_MODEL), BETA * ATT_OUT ** -0.5),
        'w_out': nrm(ks[11], (D_MODEL, D_MODEL), BETA * D_MODEL ** -0.5),
        'ln1_g': 1.0 + nrm(ks[12], (D_MODEL,), 0.02),
        'ln1_b': nrm(ks[13], (D_MODEL,), 0.02),
        'w_up': nrm(ks[14], (D_MODEL, 2 * D_FF), BETA * D_MODEL ** -0.5),
        'w_conv_ffn': nrm(ks[15], (CONV_W, D_FF), CONV_W ** -0.5),
        'w_down': nrm(ks[16], (D_FF, D_MODEL), BETA * D_FF ** -0.5),
        'ln2_g': 1.0 + nrm(ks[17], (D_MODEL,), 0.02),
        'ln2_b': nrm(ks[18], (D_MODEL,), 0.02),
    }


def reference(x_prompt, x_sample, state_conv_mix, cache_kv0, cache_kv1, cache_kv2, state_conv_ffn,
              w_in, w_conv_mix, w_branch_a, w_branch_b, w_out, ln1_g, ln1_b,
              w_up, w_conv_ffn, w_down, ln2_g, ln2_b):
    weights = (w_in, w_conv_mix, w_branch_a, w_branch_b, w_out, ln1_g, ln1_b,
               w_up, w_conv_ffn, w_down, ln2_g, ln2_b)
    nb = x_prompt.shape[0]
    zeros_mix = jnp.zeros((nb, CONV_W - 1, D_CONV), x_prompt.dtype)
    zeros_ffn = jnp.zeros((nb, CONV_W - 1, D_FF), x_prompt.dtype)
    y_prompt, cm_p, kv_p, cf_p = _layer(x_prompt, zeros_mix, None, zeros_ffn, *weights)
    y_sample, cm_s, kv_s, cf_s = _layer(x_sample, state_conv_mix, (cache_kv0, cache_kv1, cache_kv2),
                                        state_conv_ffn, *weights)
    return (y_prompt, y_sample, cm_p, kv_p[0], kv_p[1], kv_p[2], cf_p,
            cm_s, kv_s[0], kv_s[1], kv_s[2], cf_s)
```

```python
import numpy as np
from contextlib import ExitStack
import concourse.bass as bass
import concourse.mybir as mybir
from concourse.bass_utils import run_bass_kernel_spmd

F32 = mybir.dt.float32
BF16 = mybir.dt.bfloat16
U8 = mybir.dt.uint8
ALU = mybir.AluOpType
AF = mybir.ActivationFunctionType

S = 2048
HD = 128
HPG = 4
NG = 3
GROUPS = ((128, 1), (512, 4), (2048, 16))
ATT = 1536
SCALE = HD ** -0.5
NEG = -10000.0
EPS = 1e-5
ALPHA = 2.0 ** 0.25
DS = 4
NCORES = 8


class Cfg:
    def __init__(self, D=2048, DF=5632, NP=2, NSQ=4, FG=8, sample=True):
        self.D, self.DF, self.NP, self.NSQ, self.FG, self.sample = D, DF, NP, NSQ, FG, sample
        self.KC = D // 128
        self.FC = DF // 128
        self.E = 5 * D + 3 * ATT
        self.oH, self.oBG, self.oCG = 0, D, 2 * D
        self.oQ, self.oK, self.oV = 3 * D, 3 * D + ATT, 3 * D + 2 * ATT
        self.oGA, self.oGB = 3 * D + 3 * ATT, 4 * D + 3 * ATT
        self.NT = 16
        assert NSQ * DS == self.NT


class Op:
    __slots__ = ("eng", "fn", "deps", "dma", "inc", "sem", "val", "waits")

    def __init__(self, eng, fn, deps, dma):
        self.eng, self.fn, self.deps, self.dma = eng, fn, deps, dma
        self.inc = False
        self.sem = None
        self.val = 0
        self.waits = None


class Em:
    ENGS = ("pe", "act", "dve", "pool", "sp")
    RING = 8

    def __init__(self):
        self.ops = []
        self.lastw = {}
        self.rd = {}
        self.dry = False
        self.last_on = {}
        self.dma_all = []
        self.lookahead = True

    def add(self, eng, fn, r=(), w=(), dma=None):
        if self.dry:
            return
        deps = set()
        for k in r:
            x = self.lastw.get(k)
            if x is not None:
                deps.add(x)
        for k in w:
            x = self.lastw.get(k)
            if x is not None:
                deps.add(x)
            for y in self.rd.get(k, ()):
                deps.add(y)
        i = len(self.ops)
        self.ops.append(Op(eng, fn, deps, dma))
        for k in w:
            self.lastw[k] = i
            self.rd[k] = []
        for k in r:
            lst = self.rd.setdefault(k, [])
            if dma is None:
                lst[:] = [y for y in lst if not (self.ops[y].dma is None and self.ops[y].eng == eng)]
            lst.append(i)
        if dma is None:
            self.last_on[eng] = i
        else:
            self.dma_all.append(i)

    def barrier(self):
        if self.dry:
            return
        deps = set(self.last_on.values()) | set(i for i in self.dma_all if self.ops[i].dma not in ("pro", "kv"))
        for e in self.ENGS:
            self.ops.append(Op(e, None, set(deps), None))
        keep = {k: v for k, v in self.lastw.items() if k[0] == "wscr"}
        self.lastw.clear()
        self.lastw.update(keep)
        self.rd.clear()
        self.dma_all = [i for i in self.dma_all if self.ops[i].dma in ("pro", "kv")]
        for e in self.ENGS:
            self.last_on[e] = len(self.ops) - len(self.ENGS) + self.ENGS.index(e)

    def plan(self):
        ops = self.ops
        ring_hist = {}
        for i, o in enumerate(ops):
            if o.dma is not None:
                h = ring_hist.setdefault(o.dma, [])
                if len(h) >= self.RING:
                    o.deps.add(h[len(h) - self.RING])
                h.append(i)
        self.ring_hist = ring_hist
        for i, o in enumerate(ops):
            for d in o.deps:
                y = ops[d]
                if y.dma is None and not (y.eng == "pe" and o.eng == "pe" and o.dma is None):
                    y.inc = True
        cnt = {e: 0 for e in self.ENGS}
        ringcnt = {}
        for i, o in enumerate(ops):
            if o.dma is not None:
                k = ringcnt.get(o.dma, 0)
                ringcnt[o.dma] = k + 1
                o.sem = ("ring", o.dma, k % self.RING)
                o.val = 16 * (k // self.RING + 1)
            elif o.inc and o.fn is not None:
                cnt[o.eng] += 1
                o.sem = ("eng", o.eng)
                o.val = cnt[o.eng]
            else:
                o.sem = ("eng", o.eng)
                o.val = cnt[o.eng]
        waited = {e: {} for e in self.ENGS}

        def needs_of(i, before=None):
            o = ops[i]
            need = {}
            for d in o.deps:
                if before is not None and d >= before:
                    continue
                y = ops[d]
                if y.dma is None:
                    if y.eng == "pe" and o.eng == "pe" and o.dma is None:
                        continue
                    if y.fn is None and y.eng == o.eng:
                        continue
                    if y.val == 0:
                        continue
                if y.sem not in need or need[y.sem] < y.val:
                    need[y.sem] = y.val
            return need

        pe_list = [i for i, o in enumerate(ops) if o.eng == "pe" and o.dma is None and o.fn is not None]
        nxt_pe = {pe_list[k]: pe_list[k + 1] for k in range(len(pe_list) - 1)}
        for i, o in enumerate(ops):
            need = needs_of(i)
            if self.lookahead and i in nxt_pe:
                for s_, v_ in needs_of(nxt_pe[i], before=i).items():
                    if s_[0] == "ring":
                        continue
                    if s_ not in need or need[s_] < v_:
                        need[s_] = v_
            wl = []
            wd = waited[o.eng]
            for s, v in need.items():
                if wd.get(s, 0) >= v:
                    continue
                wd[s] = v
                wl.append((s, v))
            o.waits = wl
        self.final = []
        for name, h in ring_hist.items():
            n = len(h)
            for slot in range(min(n, self.RING)):
                k_last = ((n - 1 - slot) // self.RING) * self.RING + slot
                self.final.append((("ring", name, slot), 16 * (k_last // self.RING + 1)))
        for e in ("pe", "act", "dve", "pool"):
            if cnt[e]:
                self.final.append((("eng", e), cnt[e]))
        self.cnt = cnt
        print("waits per engine:", {e: sum(len(o.waits) for o in ops if o.eng == e) for e in self.ENGS})

    def emit(self, nc, es):
        sems = {}

        def sem(key):
            if key not in sems:
                sems[key] = es.enter_context(nc.semaphore("s_" + "_".join(str(x) for x in key)))
            return sems[key]

        for o in self.ops:
            sem(o.sem)
            for s, _ in o.waits:
                sem(s)
        for s, _ in self.final:
            sem(s)
        per = {e: [o for o in self.ops if o.eng == e] for e in self.ENGS}
        final = self.final

        def run(e, eng_name):
            for o in per[eng_name]:
                for s, v in o.waits:
                    e.wait_ge(sems[s], v)
                if o.fn is None:
                    continue
                ins = o.fn(e)
                if o.dma is not None:
                    ins.then_inc(sems[o.sem], 16)
                elif o.inc:
                    ins.then_inc(sems[o.sem], 1)
            if eng_name == "sp":
                for s, v in final:
                    e.wait_ge(sems[s], v)

        with nc.Block() as block:
            @block.tensor
            def _(e):
                run(e, "pe")

            @block.scalar
            def _(e):
                run(e, "act")

            @block.vector
            def _(e):
                run(e, "dve")

            @block.gpsimd
            def _(e):
                run(e, "pool")

            @block.sync
            def _(e):
                run(e, "sp")


def _slopes():
    h = np.arange(1, 13, dtype=np.float64)
    return np.exp2(-8.0 * h / 12.0).reshape(3, 4)


def make_consts():
    sl = _slopes()
    k = np.arange(128)[:, None].astype(np.float64)
    q = np.arange(128)[None, :].astype(np.float64)
    biasT = np.zeros((4, 128, 3, 256), np.float32)
    for g, (_, dil) in enumerate(GROUPS):
        for j in range(4):
            c = sl[g, j] * dil
            du0 = q - k
            t0 = np.where(du0 >= 0, -c * du0, NEG)
            du1 = 128 + q - k
            t1 = np.where(du1 <= 128, -c * du1, NEG)
            biasT[j, :, g, 0:128] = t0
            biasT[j, :, g, 128:256] = t1
    sbias = np.zeros((128, 3, 4, 4), np.float32)
    r = np.arange(128).astype(np.float64)
    for g, (_, dil) in enumerate(GROUPS):
        for h in range(4):
            c = sl[g, h] * dil
            for t in range(4):
                if g == 0:
                    sbias[:, g, h, t] = np.where(r >= t, -c * (128 + t - r), NEG)
                else:
                    sbias[:, g, h, t] = -c * (128 - r)
    nbias = np.full((16, 3, 4, 16), NEG, np.float32)
    for g, (_, dil) in enumerate(GROUPS):
        for h in range(4):
            c = sl[g, h] * dil
            for b in range(4):
                for s in range(4):
                    for t in range(4):
                        ok = (s <= t) if g == 0 else (s == t)
                        if ok:
                            nbias[4 * b + s, g, h, 4 * b + t] = -c * (t - s)
    ident = np.eye(128, dtype=np.float32)
    return dict(c_ident=ident, c_biasT=biasT.reshape(4, 128, 768),
                c_sbias=sbias.reshape(128, 48), c_nbias=nbias.reshape(16, 192))


def build_program(cfg):
    D, DF, KC, FC, NP, NSQ, NT = cfg.D, cfg.DF, cfg.KC, cfg.FC, cfg.NP, cfg.NSQ, cfg.NT
    nc = bass.Bass("TRN2", target_bir_lowering=False)

    def din(name, shape, dt=F32):
        return nc.dram_tensor(name, list(shape), dt, kind="ExternalInput").ap()

    def dout(name, shape, dt=F32):
        return nc.dram_tensor(name, list(shape), dt, kind="ExternalOutput").ap()

    x_p = din("x_p", [NP * S, D])
    x_s = din("x_s", [NT, D])
    st_mix = din("st_mix", [NSQ * 2, D])
    st_ffn = din("st_ffn", [NSQ * 2, DF])
    caches = [din("c%d" % g, [NSQ, GROUPS[g][0], 1024]) for g in range(3)]
    w_in = din("w_in", [D, cfg.E])
    w_cm = din("w_conv_mix", [3, D])
    w_a = din("w_branch_a", [D, D])
    w_b = din("w_branch_b", [512, D])
    w_o = din("w_out", [D, D])
    ln1g, ln1b = din("ln1_g", [1, D]), din("ln1_b", [1, D])
    w_up = din("w_up", [D, 2 * DF])
    w_cf = din("w_conv_ffn", [3, DF])
    w_dn = din("w_down", [DF, D])
    ln2g, ln2b = din("ln2_g", [1, D]), din("ln2_b", [1, D])
    c_ident = din("c_ident", [128, 128])
    c_biasT = din("c_biasT", [4, 128, 768])
    c_sbias = din("c_sbias", [128, 48])
    c_nbias = din("c_nbias", [16, 192])

    y_p = dout("y_p", [NP * S, D])
    y_s = dout("y_s", [NT, D])
    cm_p = dout("cm_p", [NP, 2, D])
    kv_p = [dout("kv%d_p" % g, [NP, GROUPS[g][0], 2, 4, 128]) for g in range(3)]
    cf_p = dout("cf_p", [NP, 2, DF])
    cm_s = dout("cm_s", [NSQ, 2, D])
    kv_s = [dout("kv%d_s" % g, [NSQ, GROUPS[g][0], 1024]) for g in range(3)]
    cf_s = dout("cf_s", [NSQ, 2, DF])

    blocks = {}
    order = []

    def defblock(name, pieces, kcn):
        ncols = sum(p.shape[1] for p in pieces)
        blocks[name] = (len(order), kcn, ncols, pieces)
        order.append(name)

    for j in range(4):
        for kind, off in (("Q", cfg.oQ), ("K", cfg.oK), ("V", cfg.oV)):
            defblock("%s%d" % (kind, j),
                     [w_in[:, off + g * 512 + j * 128: off + g * 512 + (j + 1) * 128] for g in range(3)], KC)
    NIB = D // 512
    for i in range(NIB):
        defblock("H%d" % i, [w_in[:, cfg.oH + 512 * i: cfg.oH + 512 * (i + 1)]], KC)
        defblock("C%d" % i, [w_in[:, cfg.oCG + 512 * i: cfg.oCG + 512 * (i + 1)]], KC)
        defblock("B%d" % i, [w_in[:, cfg.oBG + 512 * i: cfg.oBG + 512 * (i + 1)]], KC)
    for i in range(NIB):
        defblock("GA%d" % i, [w_in[:, cfg.oGA + 512 * i: cfg.oGA + 512 * (i + 1)]], KC)
        defblock("WA%d" % i, [w_a[:, 512 * i: 512 * (i + 1)]], KC)
        defblock("GB%d" % i, [w_in[:, cfg.oGB + 512 * i: cfg.oGB + 512 * (i + 1)]], KC)
        defblock("WB%d" % i, [w_b[:, 512 * i: 512 * (i + 1)]], 4)
    for n in range(NIB):
        defblock("WO%d" % n, [w_o[:, 512 * n: 512 * (n + 1)]], KC)
    NFB = DF // 512
    for i in range(NFB):
        defblock("UA%d" % i, [w_up[:, 512 * i: 512 * (i + 1)]], KC)
        defblock("UB%d" % i, [w_up[:, DF + 512 * i: DF + 512 * (i + 1)]], KC)
    fgroups = []
    c0 = 0
    while c0 < FC:
        ng = min(cfg.FG, FC - c0)
        fgroups.append((c0, ng))
        c0 += ng
    for gi, (c0, ng) in enumerate(fgroups):
        for n in range(NIB):
            defblock("WD%d_%d" % (gi, n), [w_dn[c0 * 128:(c0 + ng) * 128, 512 * n: 512 * (n + 1)]], ng)
    NBLK = len(order)
    wscr = nc.dram_tensor("wscr", [NBLK, 128, 16 * 512], BF16, kind="Internal").ap()
    xTs = nc.dram_tensor("xTs", [NP * 4, 128, KC * 512], BF16, kind="Internal").ap()
    xTsm = nc.dram_tensor("xTsm", [128, KC * NT], BF16, kind="Internal").ap()

    em = Em()

    with ExitStack() as es:
        NBYTES = 212736
        big = es.enter_context(nc.sbuf_tensor("big", [128, NBYTES], U8))
        acc = [es.enter_context(nc.psum_tensor("acc%d" % i, [128, 512], F32)) for i in range(4)]
        tp = [es.enter_context(nc.psum_tensor("tp%d" % i, [128, 512], F32)) for i in range(2)]
        sTb = [es.enter_context(nc.psum_tensor("sT%d" % i, [128, 512], F32)) for i in range(2)]

        class Alloc:
            def __init__(self, base, limit):
                self.p, self.limit = base, limit

            def get(self, shape, dt, parts=128):
                esz = 2 if dt == BF16 else 4
                n = int(np.prod(shape)) * esz
                n = (n + 63) // 64 * 64
                off = self.p
                self.p += n
                assert self.p <= self.limit, ("SBUF overflow", self.p, self.limit)
                ap = big[0:parts, off:off + int(np.prod(shape)) * esz].bitcast(dt)
                if len(shape) == 2:
                    ap = ap.rearrange("p (a b) -> p a b", a=shape[0])
                elif len(shape) == 3:
                    ap = ap.rearrange("p (a b c) -> p a b c", a=shape[0], b=shape[1])
                return ap

        al = Alloc(0, NBYTES)
        ident = al.get([128], F32)
        identb = al.get([128], BF16)
        onesb = al.get([128], BF16)
        onesf = al.get([128], F32)
        wcm = al.get([KC, 3], F32)
        wcf = al.get([FC, 3], F32)
        hu = al.get([KC, 2], F32)
        ha = al.get([FC, 2], F32)
        smix = al.get([KC, NSQ, 2], F32)
        sffn = al.get([FC, NSQ, 2], F32)
        wring = [al.get([16, 512], BF16) for _ in range(3)]
        bufA = al.get([16, 512], BF16)
        bufB = al.get([16, 512], BF16)
        ybT = al.get([4, S], BF16)
        ybTs = al.get([4, NT], BF16)
        phase_base = al.p

        def K_(name, *idx):
            return (name,) + idx

        accn = [0]
        tpn = [0]

        def next_acc():
            i = accn[0] % 4
            accn[0] += 1
            return i

        def next_acc2():
            i = accn[0] % 2
            accn[0] += 1
            return i

        def next_tp():
            i = tpn[0] % 2
            tpn[0] += 1
            return i

        evn = [0]

        def copy_op(out, in_, r, w, eng=None):
            if out.dtype == F32 and eng != "pool":
                eng = "act"
            if eng is None:
                eng = "act" if evn[0] % 2 == 0 else "dve"
                evn[0] += 1
            if eng == "act":
                em.add("act", lambda e: e.activation(out=out, in_=in_, func=AF.Copy), r=r, w=w)
            elif eng == "dve":
                em.add("dve", lambda e: e.tensor_copy(out=out, in_=in_), r=r, w=w)
            else:
                em.add("pool", lambda e: e.tensor_copy(out=out, in_=in_), r=r, w=w)

        def mm(out, pairs, r, w, skip=False, first_start=True):
            def fn(e):
                n = len(pairs)
                ins = None
                for i, (a, b) in enumerate(pairs):
                    if skip:
                        ins = e.matmul(out, lhsT=a, rhs=b, start=(first_start and i == 0), stop=(i == n - 1),
                                       skip_group_check=True)
                    else:
                        ins = e.matmul(out, lhsT=a, rhs=b, start=(i == 0), stop=(i == n - 1))
                return ins
            em.add("pe", fn, r=r, w=w)

        def tr(out, in_, idn, r, w):
            em.add("pe", lambda e: e.transpose(out, in_, idn), r=r, w=w)

        def dma(eng, out, in_, r, w, ring, slow=False, **kw):
            if slow:
                em.add(eng, lambda e: e.dma_start(out=out, in_=in_, allow_slow_non_contiguous=True, **kw),
                       r=r, w=w, dma=ring)
            else:
                em.add(eng, lambda e: e.dma_start(out=out, in_=in_, **kw), r=r, w=w, dma=ring)

        stgk = [0]

        def load_fm_table(tab, src, R, nchk, stages, key):
            HC = 16
            for h0 in range(0, nchk, HC):
                hn = min(HC, nchk - h0)
                si = stgk[0] % len(stages)
                stgk[0] += 1
                sb = stages[si]
                sk = K_("fmstage", si)
                dma("sp", sb[0:R, 0:hn * 128], src[:, h0 * 128:(h0 + hn) * 128], r=[], w=[sk], ring="io")
                for c4 in range(0, hn, 4):
                    n = min(4, hn - c4)
                    t = next_tp()
                    for j in range(n):
                        tr(tp[t][:, j * R:(j + 1) * R], sb[0:R, (c4 + j) * 128:(c4 + j + 1) * 128], ident[0:R, 0:R],
                           r=[sk, K_("ident")], w=[K_("tp", t)])
                    copy_op(tab[:, h0 + c4:h0 + c4 + n, :], tp[t][:, 0:n * R].rearrange("p (a b) -> p a b", a=n),
                            r=[K_("tp", t)], w=[key])

        def store_fm_table(tab, dst, R, nchk, stage_of, stage_keys, key):
            for c4 in range(0, nchk, 4):
                n = min(4, nchk - c4)
                t = next_tp()
                for j in range(n):
                    kk_ = key(c4 + j) if callable(key) else key
                    tr(tp[t][0:R, j * 128:(j + 1) * 128], tab[:, c4 + j, :], ident, r=[kk_, K_("ident")], w=[K_("tp", t)])
                si = (c4 // 4) % len(stage_keys)
                sb = stage_of(si)
                copy_op(sb[0:R, 0:n * 128], tp[t][0:R, 0:n * 128], r=[K_("tp", t)], w=[stage_keys[si]])
                dma("sp", dst[:, c4 * 128:(c4 + n) * 128], sb[0:R, 0:n * 128], r=[stage_keys[si]], w=[], ring="io")

        class WS:
            def __init__(self):
                self.seq = []
                self.rec = True
                self.i = 0
                self.loaded = 0
                self.PF = 2

            def _load(self, k):
                name = self.seq[k]
                idx, kcn, ncols, _ = blocks[name]
                slot = k % 3
                dst = wring[slot]
                dma("sp", dst.rearrange("p a b -> p (a b)")[:, 0:kcn * ncols],
                    wscr[idx][:, 0:kcn * ncols],
                    r=[K_("wscr", idx)], w=[K_("w", slot)], ring="w")

            def prefetch(self, n):
                if self.rec:
                    return
                while self.loaded <= min(self.i - 1 + n, len(self.seq) - 1):
                    self._load(self.loaded)
                    self.loaded += 1

            def get(self, name, pf=None):
                if self.rec:
                    self.seq.append(name)
                    return None, None
                k = self.i
                assert self.seq[k] == name
                self.i += 1
                pf = self.PF if pf is None else pf
                while self.loaded <= min(k + pf, len(self.seq) - 1):
                    self._load(self.loaded)
                    self.loaded += 1
                slot = k % 3
                kcn, ncols = blocks[name][1], blocks[name][2]
                return wring[slot].rearrange("p a b -> p (a b)")[:, 0:kcn * ncols].rearrange(
                    "p (a b) -> p a b", a=kcn), K_("w", slot)

        ws = WS()

        def prologue():
            dma("sp", ident, c_ident, r=[], w=[K_("ident")], ring="io")
            em.add("dve", lambda e: e.tensor_copy(out=identb, in_=ident), r=[K_("ident")], w=[K_("identb")])
            em.add("pool", lambda e: e.memset(onesb, 1.0), r=[], w=[K_("onesb")])
            em.add("pool", lambda e: e.memset(onesf, 1.0), r=[], w=[K_("onesf")])
            for name in order:
                idx, kcn, ncols, pieces = blocks[name]
                dstv = wscr[idx][:, 0:kcn * ncols].rearrange("p (a b) -> p a b", a=kcn)
                c = 0
                for pc in pieces:
                    wdt = pc.shape[1]
                    dma("pool", dstv[:, :, c:c + wdt], pc.rearrange("(kc p) n -> p kc n", p=128),
                        r=[], w=[K_("wscr", idx)], ring="pro")
                    c += wdt
        def prologue2():
            pass

        kvn = [0]

        def kv_shift():
            if cfg.sample:
                for g in range(3):
                    L = GROUPS[g][0]
                    n = L - DS
                    step = 512
                    for b in range(NSQ):
                        for r0 in range(0, n, step):
                            r1 = min(n, r0 + step)
                            dma("pool", kv_s[g][b, r0:r1, :], caches[g][b, DS + r0:DS + r1, :], r=[],
                                w=[K_("kvchain", kvn[0] % 2)], ring="kv")
                            kvn[0] += 1

        def phase0(sq0):
            al0 = Alloc(phase_base, NBYTES)
            xs = [al0.get([D], F32) for _ in range(2)]
            xsn = 0
            G4 = min(4, KC)
            for sq in range(sq0, sq0 + 1):
                for tt in range(4):
                    buf = bufA if (sq * 4 + tt) % 2 == 0 else bufB
                    bk = "A" if (sq * 4 + tt) % 2 == 0 else "B"
                    for s in range(4):
                        xb = xs[xsn % 2]
                        xk = K_("xs", xsn % 2)
                        xsn += 1
                        r0 = sq * S + tt * 512 + s * 128
                        dma("sp", xb, x_p[r0:r0 + 128, :], r=[], w=[xk], ring="io")
                        for k4 in range(0, KC, G4):
                            t = next_tp()
                            for j in range(G4):
                                tr(tp[t][:, j * 128:(j + 1) * 128], xb[:, (k4 + j) * 128:(k4 + j + 1) * 128], ident,
                                   r=[xk, K_("ident")], w=[K_("tp", t)])
                            copy_op(buf[:, k4:k4 + G4, s * 128:(s + 1) * 128],
                                    tp[t][:, 0:G4 * 128].rearrange("p (a b) -> p a b", a=G4),
                                    r=[K_("tp", t)], w=[K_(bk, "all")])
                    dma("sp", xTs[sq * 4 + tt][:, 0:KC * 512].rearrange("p (a b) -> p a b", a=KC), buf[:, 0:KC, :],
                        r=[K_(bk, "all")], w=[K_("xTs", sq * 4 + tt)], ring="io")
            if sq0 != 0:
                return
            stg0 = [al0.get([2048], F32) for _ in range(2)]
            load_fm_table(wcm, w_cm, 3, KC, stg0, K_("wctab"))
            load_fm_table(wcf, w_cf, 3, FC, stg0, K_("wctab"))
            if cfg.sample:
                xb = xs[xsn % 2]
                xk = K_("xs", xsn % 2)
                dma("sp", xb[0:NT, :], x_s, r=[], w=[xk], ring="io")
                for k4 in range(0, KC, G4):
                    t = next_tp()
                    for j in range(G4):
                        tr(tp[t][:, j * NT:(j + 1) * NT], xb[0:NT, (k4 + j) * 128:(k4 + j + 1) * 128], ident[0:NT, 0:NT],
                           r=[xk, K_("ident")], w=[K_("tp", t)])
                    copy_op(bufA[:, k4:k4 + G4, 0:NT], tp[t][:, 0:G4 * NT].rearrange("p (a b) -> p a b", a=G4),
                            r=[K_("tp", t)], w=[K_("A", "all")])
                dma("sp", xTsm.rearrange("p (a b) -> p a b", a=KC), bufA[:, 0:KC, 0:NT],
                    r=[K_("A", "all")], w=[K_("xTsm")], ring="io")

        def phase1(sq):
            al1 = Alloc(phase_base, NBYTES)
            qT = al1.get([3, S], BF16)
            kT = al1.get([3, S], BF16)
            vT = al1.get([3, S], BF16)
            kf = al1.get([3, 512], F32)
            vf = al1.get([3, 512], F32)
            vB = al1.get([3, 16, 128], BF16)
            an = al1.get([S], F32)
            ad = al1.get([S], F32)
            pT = [al1.get([256], BF16) for _ in range(6)]
            stmp = [al1.get([256], F32) for _ in range(2)]
            stg = [al1.get([4, 128], F32) for _ in range(3)]
            bias3 = al1.get([3, 256], F32)
            xs1 = [al1.get([D], F32) for _ in range(2)]
            xs1n = [0]
            stgn = [0]
            xtn = [0]
            p1s = getattr(cfg, "p1stop", 99)
            for j in range(4):
                dma("sp", bias3.rearrange("p a b -> p (a b)"), c_biasT[j], r=[], w=[K_("bias3")], ring="io")
                em.add("dve", lambda e: e.memset(an, 0.0), r=[], w=[K_("an")])
                em.add("dve", lambda e: e.memset(ad, 0.0), r=[], w=[K_("ad")])
                wq, kq = ws.get("Q%d" % j, pf=0)
                wk, kk = ws.get("K%d" % j, pf=0)
                wv, kv = ws.get("V%d" % j, pf=0)
                for tt in range(4):
                    xb = bufA if xtn[0] % 2 == 0 else bufB
                    xk = K_("A" if xtn[0] % 2 == 0 else "B", "all")
                    xtn[0] += 1
                    if j == 0 and sq == 0 and tt == 0:
                        dma("sp", xb[:, 0:KC, :], xTs[sq * 4 + tt][:, 0:KC * 512].rearrange("p (a b) -> p a b", a=KC),
                            r=[K_("xTs", sq * 4 + tt)], w=[xk], ring="io")
                    elif j == 0 and sq > 0:
                        G4 = min(4, KC)
                        for s_ in range(4):
                            xsb = xs1[xs1n[0] % 2]
                            xsk = K_("xs1", xs1n[0] % 2)
                            xs1n[0] += 1
                            r0 = sq * S + tt * 512 + s_ * 128
                            dma("sp", xsb, x_p[r0:r0 + 128, :], r=[], w=[xsk], ring="io")
                            for k4 in range(0, KC, G4):
                                t = next_tp()
                                for jj in range(G4):
                                    tr(tp[t][:, jj * 128:(jj + 1) * 128], xsb[:, (k4 + jj) * 128:(k4 + jj + 1) * 128], ident,
                                       r=[xsk, K_("ident")], w=[K_("tp", t)])
                                copy_op(xb[:, k4:k4 + G4, s_ * 128:(s_ + 1) * 128],
                                        tp[t][:, 0:G4 * 128].rearrange("p (a b) -> p a b", a=G4),
                                        r=[K_("tp", t)], w=[xk])
                        dma("sp", xTs[sq * 4 + tt][:, 0:KC * 512].rearrange("p (a b) -> p a b", a=KC), xb[:, 0:KC, :],
                            r=[xk], w=[K_("xTs", sq * 4 + tt)], ring="io")
                    nj, ntt = (j, tt + 1) if tt < 3 else (j + 1, 0)
                    if (nj >= 1 or sq == 0) and nj < 4:
                        nxb = bufA if xtn[0] % 2 == 0 else bufB
                        nxk = K_("A" if xtn[0] % 2 == 0 else "B", "all")
                        dma("sp", nxb[:, 0:KC, :], xTs[sq * 4 + ntt][:, 0:KC * 512].rearrange("p (a b) -> p a b", a=KC),
                            r=[K_("xTs", sq * 4 + ntt)], w=[nxk], ring="io")
                    for kind, wt, wkey, dst, fdst in (("Q", wq, kq, qT, None), ("K", wk, kk, kT, kf), ("V", wv, kv, vT, vf)):
                        if p1s < 1:
                            continue
                        import os as _os
                        if _os.environ.get("P1KIND") and kind not in _os.environ.get("P1KIND"):
                            continue
                        for c in range(3):
                            a = next_acc()
                            if not em.dry:
                                mm(acc[a][:, 0:512],
                                   [(wt[:, kc, c * 128:(c + 1) * 128], xb[:, kc, :]) for kc in range(KC)],
                                   r=[wkey, xk], w=[K_("acc", a)])
                            copy_op(dst[:, c, tt * 512:(tt + 1) * 512], acc[a][:, 0:512],
                                    r=[K_("acc", a)], w=[K_(kind + "T", c, tt)], eng="act")
                            if fdst is not None:
                                win = GROUPS[c][0]
                                if (tt + 1) * 512 > S - win:
                                    copy_op(fdst[:, c, :], acc[a][:, 0:512], r=[K_("acc", a)], w=[K_(kind + "f", c)],
                                            eng="act")
                        if fdst is not None and p1s >= 2:
                            which = 0 if kind == "K" else 1
                            for g in range(3):
                                win = GROUPS[g][0]
                                ms = [m for m in range(4) if tt * 512 + m * 128 >= S - win]
                                if not ms:
                                    continue
                                t = next_tp()
                                for m in ms:
                                    tr(tp[t][:, m * 128:(m + 1) * 128], fdst[:, g, m * 128:(m + 1) * 128], ident,
                                       r=[K_(kind + "f", g), K_("ident")], w=[K_("tp", t)])
                                sg = stgn[0] % 3
                                stgn[0] += 1
                                m0, nm = ms[0], len(ms)
                                copy_op(stg[sg][:, m0:m0 + nm, :],
                                        tp[t][:, m0 * 128:(m0 + nm) * 128].rearrange("p (a b) -> p a b", a=nm),
                                        r=[K_("tp", t)], w=[K_("stg", sg)])
                                row0 = tt * 512 + m0 * 128 - (S - win)
                                dma("sp", kv_p[g][sq, row0:row0 + nm * 128, which, j, :].rearrange("(m p) d -> p m d", p=128),
                                    stg[sg][:, m0:m0 + nm, :], r=[K_("stg", sg)], w=[], ring="io")
                ws.prefetch(3)
                if p1s < 3:
                    continue
                for g, (win, dil) in enumerate(GROUPS):
                    nblk = 16 // dil
                    for k4 in range(0, 16, 4):
                        t = next_tp()
                        tpb = tp[t][:, :].bitcast(BF16)
                        for i4 in range(4):
                            kbi = k4 + i4
                            rcl, b = kbi // nblk, kbi % nblk
                            st = rcl + dil * 128 * b
                            tr(tpb[:, i4 * 128:(i4 + 1) * 128], vT[:, g, st:st + dil * 127 + 1:dil], identb,
                               r=[K_("VT", g, tt_) for tt_ in range(4)] + [K_("identb")], w=[K_("tp", t)])
                        copy_op(vB[:, g, k4:k4 + 4, :], tpb[:, 0:512].rearrange("p (a b) -> p a b", a=4),
                                r=[K_("tp", t)], w=[K_("vB", g, k4 // 4)])
                if p1s < 4:
                    continue
                items = []
                for g, (win, dil) in enumerate(GROUPS):
                    nblk = 16 // dil
                    for rcl in range(dil):
                        for b in range(nblk):
                            items.append((g, dil, nblk, rcl, b))
                NPT = 6
                LA = 2
                state = {"bank": None, "pend": [], "prev": None, "bkn": 0}

                def emit_S(idx):
                    g, dil, nblk, rcl, b = items[idx]
                    ncols = 256 if b < nblk - 1 else 128
                    st = rcl + dil * 128 * b
                    hs = idx % 2
                    sTh = sTb[hs][:, 0:ncols]
                    qkeys = [K_("QT", g, t_) for t_ in range(4)]
                    kkeys = [K_("KT", g, t_) for t_ in range(4)]
                    mm(sTh, [(kT[:, g, st:st + dil * 127 + 1:dil], qT[:, g, st:st + dil * (ncols - 1) + 1:dil])],
                       r=qkeys + kkeys, w=[K_("sT", hs)])
                    sm = stmp[hs]
                    smk = K_("stmp", hs)
                    em.add("dve", lambda e, o=sm[:, 0:ncols], i=sTh, bb=bias3[:, g, 0:ncols]:
                           e.scalar_tensor_tensor(out=o, in0=i, scalar=SCALE, in1=bb, op0=ALU.mult, op1=ALU.add),
                           r=[K_("sT", hs), K_("bias3")], w=[smk])
                    ps = idx % NPT
                    em.add("act", lambda e, o=pT[ps][:, 0:ncols], i=sm[:, 0:ncols]:
                           e.activation(out=o, in_=i, func=AF.Exp), r=[smk], w=[K_("pT", ps)])

                def emit_PV(idx):
                    g, dil, nblk, rcl, b = items[idx]
                    ps = idx % NPT
                    kbi = rcl * nblk + b
                    if b == 0:
                        state["prev"] = None
                    if state["bank"] is None:
                        state["bank"] = (state["bkn"] % 2, 2 + state["bkn"] % 2)
                        state["bkn"] += 1
                        state["pend"] = []
                    bank = state["bank"]
                    qi = len(state["pend"])
                    pairs_n, pairs_d, rk = [], [], [K_("pT", ps), K_("vB", g, kbi // 4), K_("onesb")]
                    if state["prev"] is not None:
                        pps, pkbi = state["prev"]
                        pairs_n.append((vB[:, g, pkbi, :], pT[pps][:, 128:256]))
                        pairs_d.append((onesb, pT[pps][:, 128:256]))
                        rk += [K_("pT", pps), K_("vB", g, pkbi // 4)]
                    pairs_n.append((vB[:, g, kbi, :], pT[ps][:, 0:128]))
                    pairs_d.append((onesb, pT[ps][:, 0:128]))
                    mm(acc[bank[0]][:, qi * 128:(qi + 1) * 128], pairs_n, r=rk, w=[K_("acc", bank[0])])
                    mm(acc[bank[1]][:, qi * 128:(qi + 1) * 128], pairs_d, r=rk, w=[K_("acc", bank[1])])
                    state["pend"].append((rcl, b))
                    state["prev"] = (ps, kbi)
                    if len(state["pend"]) == 4:
                        r0, b0 = state["pend"][0]
                        if g == 0:
                            dn = an[:, b0 * 128:(b0 + 4) * 128]
                            dd = ad[:, b0 * 128:(b0 + 4) * 128]
                            sn = acc[bank[0]][:, 0:512]
                            sd = acc[bank[1]][:, 0:512]
                        elif g == 1:
                            dn = an[:, r0:S:4]
                            dd = ad[:, r0:S:4]
                            sn = acc[bank[0]][:, 0:512]
                            sd = acc[bank[1]][:, 0:512]
                        else:
                            dn = an.rearrange("p (i r) -> p r i", r=16)[:, r0:r0 + 4, :]
                            dd = ad.rearrange("p (i r) -> p r i", r=16)[:, r0:r0 + 4, :]
                            sn = acc[bank[0]][:, 0:512].rearrange("p (a b) -> p a b", a=4)
                            sd = acc[bank[1]][:, 0:512].rearrange("p (a b) -> p a b", a=4)
                        em.add("dve", lambda e, o=dn, i=sn: e.tensor_tensor(out=o, in0=o, in1=i, op=ALU.add),
                               r=[K_("acc", bank[0]), K_("an")], w=[K_("an")])
                        em.add("dve", lambda e, o=dd, i=sd: e.tensor_tensor(out=o, in0=o, in1=i, op=ALU.add),
                               r=[K_("acc", bank[1]), K_("ad")], w=[K_("ad")])
                        state["bank"] = None
                        state["pend"] = []

                for i_ in range(len(items) + LA):
                    if i_ < len(items):
                        emit_S(i_)
                    if i_ - LA >= 0 and p1s >= 5:
                        emit_PV(i_ - LA)
                assert state["bank"] is None or p1s < 5
                if p1s < 6:
                    continue
                em.add("dve", lambda e: e.reciprocal(out=ad, in_=ad), r=[K_("ad")], w=[K_("ad")])
                em.add("dve", lambda e, j=j: e.tensor_tensor(out=ybT[:, j, :], in0=an, in1=ad, op=ALU.mult),
                       r=[K_("an"), K_("ad")], w=[K_("ybT", j)])

        def phase1s():
            al1 = Alloc(phase_base, NBYTES)
            qTs = al1.get([12, NT], F32)
            kTs = al1.get([12, NT], F32)
            vTs = al1.get([12, NT], F32)
            knew = al1.get([12, 128], F32)
            vnew = al1.get([12, 128], F32)
            kct = [al1.get([9, 512], F32) for _ in range(2)]
            vct = [al1.get([9, 512], F32)] * 2
            kcT = al1.get([9, 4, 128], F32)
            sbias = al1.get([48], F32)
            nbias = al1.get([192], F32)
            stm = [al1.get([48], F32) for _ in range(2)]
            pTs = [al1.get([48], F32) for _ in range(2)]
            stn_ = al1.get([192], F32)
            pTn = al1.get([192], F32)
            dsb = al1.get([64], F32)
            stg1 = [al1.get([2048], F32) for _ in range(2)]
            dma("sp", sbias, c_sbias, r=[], w=[K_("sbias")], ring="io")
            dma("sp", nbias[0:16, :], c_nbias, r=[], w=[K_("nbias")], ring="io")
            xb, xk = bufA, K_("A", "all")
            dma("sp", xb[:, 0:KC, 0:NT], xTsm.rearrange("p (a b) -> p a b", a=KC), r=[K_("xTsm")], w=[xk], ring="io")
            for j in range(4):
                for kind, dst in (("Q", qTs), ("K", kTs), ("V", vTs)):
                    wt, wkey = ws.get("%s%d" % (kind, j))
                    for c in range(3):
                        a = next_acc2()
                        if not em.dry:
                            mm(acc[a][:, 0:NT], [(wt[:, kc, c * 128:(c + 1) * 128], xb[:, kc, 0:NT]) for kc in range(KC)],
                               r=[wkey, xk], w=[K_("acc", a)])
                        copy_op(dst[:, c * 4 + j, :], acc[a][:, 0:NT], r=[K_("acc", a)], w=[K_("s" + kind)])
            for kind, src, dst, which in (("K", kTs, knew, 0), ("V", vTs, vnew, 1)):
                for g4 in range(3):
                    t = next_tp()
                    for i in range(4):
                        tr(tp[t][0:NT, i * 128:(i + 1) * 128], src[:, g4 * 4 + i, :], ident, r=[K_("s" + kind), K_("ident")],
                           w=[K_("tp", t)])
                    copy_op(dst[0:NT, g4 * 4:(g4 + 1) * 4, :], tp[t][0:NT, 0:512].rearrange("p (a b) -> p a b", a=4),
                            r=[K_("tp", t)], w=[K_("new" + kind)])
                for g in range(3):
                    L = GROUPS[g][0]
                    for b in range(NSQ):
                        dstv = kv_s[g][b, L - DS:L, which * 512:(which + 1) * 512].rearrange("t (j d) -> t j d", j=4)
                        dma("sp", dstv, dst[b * DS:(b + 1) * DS, g * 4:(g + 1) * 4, :], r=[K_("new" + kind)], w=[],
                            ring="io")
            for gh in range(12):
                mm(sTb[0][0:NT, gh * 16:(gh + 1) * 16], [(kTs[:, gh, :], qTs[:, gh, :])], r=[K_("sK"), K_("sQ")],
                   w=[K_("sT", 0)])
            em.add("dve", lambda e: e.scalar_tensor_tensor(out=stn_[0:NT, :], in0=sTb[0][0:NT, 0:192], scalar=SCALE,
                                                           in1=nbias[0:NT, :], op0=ALU.mult, op1=ALU.add),
                   r=[K_("sT", 0), K_("nbias")], w=[K_("stn")])
            em.add("act", lambda e: e.activation(out=pTn[0:NT, :], in_=stn_[0:NT, :], func=AF.Exp), r=[K_("stn")],
                   w=[K_("pTn")])
            NB, DB = 2, 3
            first = [True, True]
            for h in range(4):
                for g in range(3):
                    gh = g * 4 + h
                    mm(acc[NB][:, h * 16:(h + 1) * 16], [(vnew[0:NT, gh, :], pTn[0:NT, gh * 16:(gh + 1) * 16])],
                       r=[K_("newV"), K_("pTn")], w=[K_("acc", NB)], skip=True, first_start=first[0])
                    first[0] = False
                    mm(acc[DB][:, h * 16:(h + 1) * 16], [(onesf[0:NT, :], pTn[0:NT, gh * 16:(gh + 1) * 16])],
                       r=[K_("onesf"), K_("pTn")], w=[K_("acc", DB)], skip=True, first_start=first[1])
                    first[1] = False
            units = [(0, 0)] + [(1, cl) for cl in range(DS)] + [(2, cl) for cl in range(DS)]
            for b in range(NSQ):
                bb = b % 2
                for which_ in (0, 1):
                    for ui, (g, cl) in enumerate(units):
                        L, dil = GROUPS[g]
                        rows = slice(0, 128) if g == 0 else slice(cl, cl + dil * 127 + 1, dil)
                        if which_ == 0:
                            dma("sp", kct[bb][:, ui, :], caches[g][b, rows, 0:512], r=[], w=[K_("kct", bb)], ring="io")
                        else:
                            dma("sp", vct[bb][:, ui, :], caches[g][b, rows, 512:1024], r=[], w=[K_("vct")], ring="io")
                for ui, (g, cl) in enumerate(units):
                    t = next_tp()
                    for h in range(4):
                        tr(tp[t][:, h * 128:(h + 1) * 128], kct[bb][:, ui, h * 128:(h + 1) * 128], ident,
                           r=[K_("kct", bb), K_("ident")], w=[K_("tp", t)])
                    copy_op(kcT[:, ui, :, :], tp[t][:, 0:512].rearrange("p (a b) -> p a b", a=4), r=[K_("tp", t)],
                            w=[K_("kcT", ui)])
                for ui, (g, cl) in enumerate(units):
                    ts = list(range(DS)) if g == 0 else [cl]
                    nt_ = len(ts)
                    q0 = b * DS + ts[0]
                    for h in range(4):
                        col = g * 16 + h * 4 + ts[0]
                        mm(sTb[1][:, col:col + nt_], [(kcT[:, ui, h, :], qTs[:, g * 4 + h, q0:q0 + nt_])],
                           r=[K_("kcT", ui), K_("sQ")], w=[K_("sT", 1)])
                em.add("dve", lambda e, o=stm[bb], i=sTb[1][:, 0:48]:
                       e.scalar_tensor_tensor(out=o, in0=i, scalar=SCALE, in1=sbias, op0=ALU.mult, op1=ALU.add),
                       r=[K_("sT", 1), K_("sbias")], w=[K_("stm", bb)])
                em.add("act", lambda e, o=pTs[bb], i=stm[bb]: e.activation(out=o, in_=i, func=AF.Exp),
                       r=[K_("stm", bb)], w=[K_("pTs", bb)])
                for ui, (g, cl) in enumerate(units):
                    ts = list(range(DS)) if g == 0 else [cl]
                    nt_ = len(ts)
                    q0 = b * DS + ts[0]
                    for h in range(4):
                        col = g * 16 + h * 4 + ts[0]
                        mm(acc[NB][:, h * 16 + q0: h * 16 + q0 + nt_],
                           [(vct[bb][:, ui, h * 128:(h + 1) * 128], pTs[bb][:, col:col + nt_])],
                           r=[K_("vct"), K_("pTs", bb)], w=[K_("acc", NB)], skip=True, first_start=False)
                        mm(acc[DB][:, h * 16 + q0: h * 16 + q0 + nt_],
                           [(onesf, pTs[bb][:, col:col + nt_])],
                           r=[K_("onesf"), K_("pTs", bb)], w=[K_("acc", DB)], skip=True, first_start=False)
            load_fm_table(smix.rearrange("p c s i -> p c (s i)"), st_mix, NSQ * 2, KC, stg1, K_("stab"))
            load_fm_table(sffn.rearrange("p c s i -> p c (s i)"), st_ffn, NSQ * 2, FC, stg1, K_("stab"))
            copy_op(dsb, acc[DB][:, 0:64], r=[K_("acc", DB)], w=[K_("dsb")], eng="act")
            em.add("dve", lambda e: e.reciprocal(out=dsb, in_=dsb), r=[K_("dsb")], w=[K_("dsb")])
            em.add("dve", lambda e: e.tensor_tensor(out=ybTs.rearrange("p a b -> p (a b)"), in0=acc[NB][:, 0:64], in1=dsb,
                                                    op=ALU.mult),
                   r=[K_("acc", NB), K_("dsb")], w=[K_("ybTs")])

        def phase2_setup():
            al2 = Alloc(phase_base, NBYTES)
            P = {}
            P["xv"] = al2.get([4, D], F32)
            P["M"] = al2.get([16, 512], BF16)
            P["tA"] = al2.get([4, 514], F32)
            P["tB"] = al2.get([4, 512], F32)
            P["tC"] = al2.get([4, 512], F32)
            P["lng"] = al2.get([D], F32)
            P["lnb"] = al2.get([D], F32)
            P["st"] = al2.get([16, 6], F32)
            P["mv"] = al2.get([4, 8], F32)
            P["s_xT"] = al2.get([16, NT], BF16)
            P["s_ya"] = al2.get([16, NT], BF16)
            P["s_M"] = al2.get([16, NT], BF16)
            P["s_h0"] = al2.get([cfg.FG, NT], BF16)
            P["s_h1"] = al2.get([cfg.FG, NT], BF16)
            P["s_tA"] = al2.get([4, NSQ * (DS + 2)], F32)
            P["s_tB"] = al2.get([4, NT], F32)
            P["s_tC"] = al2.get([4, NT], F32)
            P["s_xv"] = al2.get([1, D], F32)
            P["s_st"] = al2.get([4, 6], F32)
            P["s_mv"] = al2.get([1, 8], F32)
            return P

        class Ctx:
            pass

        def make_ctx(P, prompt, sq=0, tt=0):
            cx = Ctx()
            cx.prompt = prompt
            if prompt:
                cx.N, cx.nseq, cx.L, cx.NS, cx.RP, cx.pre = 512, 1, 512, 4, 128, "p"
                cx.xT, cx.yaT, cx.M = bufA, bufB, P["M"]
                cx.tA, cx.tB, cx.tC, cx.xv = P["tA"], P["tB"], P["tC"], P["xv"]
                cx.hT = [(bufB[:, 0:8, :], "B", 0), (bufB[:, 8:16, :], "B", 8)]
                cx.kx, cx.kya, cx.kM = "A", "B", "M"
                cx.st, cx.mv = P["st"], P["mv"]
                cx.sq, cx.tt = sq, tt
                cx.first_tile, cx.last_tile = tt == 0, tt == 3
                cx.row0 = sq * S + tt * 512
            else:
                cx.N, cx.nseq, cx.L, cx.NS, cx.RP, cx.pre = NT, NSQ, DS, 1, NT, "s"
                cx.xT, cx.yaT, cx.M = P["s_xT"], P["s_ya"], P["s_M"]
                cx.tA, cx.tB, cx.tC, cx.xv = P["s_tA"], P["s_tB"], P["s_tC"], P["s_xv"]
                cx.hT = [(P["s_h0"], "sH0", 0), (P["s_h1"], "sH1", 0)]
                cx.kx, cx.kya, cx.kM = "sA", "sB", "sM"
                cx.st, cx.mv = P["s_st"], P["s_mv"]
                cx.first_tile = cx.last_tile = False
            return cx

        def phase2(P, ctxs, nxt=None):
            def k_(cx, name, *idx):
                return (cx.pre + name,) + idx

            def v3(cx, ap):
                return ap.rearrange("p (s l) -> p s l", s=cx.nseq)

            def halo_view(cx, j):
                return cx.tA[:, j, 0:cx.nseq * (cx.L + 2)].rearrange("p (s l) -> p s l", s=cx.nseq)

            def ln_load(which):
                gg, bb = ((ln1g, ln1b), (ln2g, ln2b))[which]
                dma("sp", P["lng"], gg.partition_broadcast(128), r=[], w=[K_("lng")], ring="io")
                dma("sp", P["lnb"], bb.partition_broadcast(128), r=[], w=[K_("lnb")], ring="io")

            def load_xT(sq_, tt_):
                dma("sp", bufA[:, 0:KC, :], xTs[sq_ * 4 + tt_][:, 0:KC * 512].rearrange("p (a b) -> p a b", a=KC),
                    r=[K_("xTs", sq_ * 4 + tt_)], w=[K_("A", c) for c in range(16)], ring="io")
                P["xT_for"] = (sq_, tt_)

            for cx in ctxs:
                if cx.prompt:
                    if P.get("xT_for") != (cx.sq, cx.tt):
                        load_xT(cx.sq, cx.tt)
                    cx.ybv = (lambda j, tt=cx.tt: ybT[:, j, tt * 512:(tt + 1) * 512])
                    cx.ybk = [K_("ybT", j) for j in range(4)]
                else:
                    dma("sp", cx.xT[:, 0:KC, :], xTsm.rearrange("p (a b) -> p a b", a=KC), r=[K_("xTsm")],
                        w=[K_(cx.kx, c) for c in range(16)], ring="io")
                    cx.ybv = (lambda j: ybTs[:, j, :])
                    cx.ybk = [K_("ybTs")]
                cx.xkeys = [K_(cx.kx, c) for c in range(KC)]
                cx.mkeys = [K_(cx.kM, c) for c in range(KC)]

            nchk = D // 512

            def ln_stats(cx, s, c):
                RP = cx.RP
                em.add("dve", lambda e: e.bn_stats(out=cx.st[0:RP, s * nchk + c, :], in_=cx.xv[0:RP, s, c * 512:(c + 1) * 512]),
                       r=[k_(cx, "xv", s, c)], w=[k_(cx, "st", s, c)])

            def ln_finish(cx):
                RP, NS = cx.RP, cx.NS
                st, mv = cx.st, cx.mv
                mk = k_(cx, "mv")
                for s in range(NS):
                    em.add("dve", lambda e, s=s: e.bn_aggr(out=mv[0:RP, s, 0:2],
                                                          in_=st[0:RP, s * nchk:(s + 1) * nchk, :].rearrange("p a b -> p (a b)")),
                           r=[k_(cx, "st", s, c) for c in range(nchk)], w=[mk])
                em.add("dve", lambda e: e.tensor_scalar_add(out=mv[0:RP, 0:NS, 2], in0=mv[0:RP, 0:NS, 1], scalar1=EPS),
                       r=[mk], w=[mk])
                em.add("act", lambda e: e.activation(out=mv[0:RP, 0:NS, 2], in_=mv[0:RP, 0:NS, 2], func=AF.Sqrt),
                       r=[mk], w=[mk])
                em.add("dve", lambda e: e.reciprocal(out=mv[0:RP, 0:NS, 2], in_=mv[0:RP, 0:NS, 2]), r=[mk], w=[mk])
                for s in range(NS):
                    for q in range(nchk):
                        xq = cx.xv[0:RP, s, q * 512:(q + 1) * 512]
                        xk = k_(cx, "xv", s, q)
                        em.add("dve", lambda e, xq=xq, s=s, q=q: e.scalar_tensor_tensor(
                            out=xq, in0=xq, scalar=mv[0:RP, s, 0:1], in1=P["lng"][0:RP, q * 512:(q + 1) * 512],
                            op0=ALU.subtract, op1=ALU.mult), r=[xk, mk, K_("lng")], w=[xk])
                        em.add("dve", lambda e, xq=xq, s=s, q=q: e.scalar_tensor_tensor(
                            out=xq, in0=xq, scalar=mv[0:RP, s, 2:3], in1=P["lnb"][0:RP, q * 512:(q + 1) * 512],
                            op0=ALU.mult, op1=ALU.add), r=[xk, mk, K_("lnb")], w=[xk])
                        yield s, q

            def wsblock(name, stage):
                wt, wkey = ws.get(name)
                kcn = blocks[name][1]
                for j4 in range(4):
                    for cx in ctxs:
                        rhs_of, rkeys, evac = stage(cx)
                        a = next_acc()
                        if not em.dry:
                            mm(acc[a][:, 0:cx.N], [(wt[:, kc, j4 * 128:(j4 + 1) * 128], rhs_of(kc)) for kc in range(kcn)],
                               r=[wkey] + rkeys, w=[K_("acc", a)])
                        evac(j4, acc[a][:, 0:cx.N], K_("acc", a))

            def conv3(cx, j, wtab, c):
                hv = halo_view(cx, j)
                L, N = cx.L, cx.N
                o = v3(cx, cx.tB[:, j, 0:N])
                srck, hk_, dstk = k_(cx, "tA", j), k_(cx, "tAh", j), k_(cx, "tB", j)
                em.add("act", lambda e: e.activation(out=o, in_=hv[:, :, 0:L], func=AF.Identity, scale=wtab[:, c, 0:1]),
                       r=[srck, hk_], w=[dstk])
                em.add("dve", lambda e: e.scalar_tensor_tensor(out=o, in0=hv[:, :, 1:L + 1], scalar=wtab[:, c, 1:2], in1=o,
                                                               op0=ALU.mult, op1=ALU.add), r=[srck, hk_, dstk], w=[dstk])
                em.add("dve", lambda e: e.scalar_tensor_tensor(out=o, in0=hv[:, :, 2:L + 2], scalar=wtab[:, c, 2:3], in1=o,
                                                               op0=ALU.mult, op1=ALU.add), r=[srck, dstk], w=[dstk])

            def set_halo(cx, j, c, htab, hkey, st_src):
                hv = halo_view(cx, j)
                if cx.prompt:
                    if cx.first_tile:
                        em.add("pool", lambda e: e.memset(hv[:, :, 0:2], 0.0), r=[], w=[k_(cx, "tAh", j)])
                    else:
                        em.add("pool", lambda e: e.tensor_copy(out=hv[:, 0, 0:2], in_=htab[:, c, :]), r=[K_(hkey, c)],
                               w=[k_(cx, "tAh", j)])
                else:
                    stab = smix if st_src is st_mix else sffn
                    em.add("pool", lambda e: e.tensor_copy(out=hv[:, :, 0:2], in_=stab[:, c, :, :]), r=[K_("stab")],
                           w=[k_(cx, "tAh", j)])

            def save_halo(cx, j, c, htab, hkey, out_p, out_s):
                hv = halo_view(cx, j)
                L = cx.L
                if cx.prompt:
                    em.add("pool", lambda e: e.tensor_copy(out=htab[:, c, :], in_=hv[:, 0, L:L + 2]),
                           r=[k_(cx, "tA", j)], w=[K_(hkey, c)])
                else:
                    stab = smix if out_s is cm_s else sffn
                    em.add("pool", lambda e: e.tensor_copy(out=stab[:, c, :, :], in_=hv[:, :, L:L + 2]),
                           r=[k_(cx, "tA", j), k_(cx, "tAh", j)], w=[K_("stab")])

            for i in range(NIB):
                def st_h(cx):
                    def ev(j4, a_ap, a_key):
                        copy_op(cx.tB[:, j4, 0:cx.N], a_ap, r=[a_key], w=[k_(cx, "tB", j4)], eng="act")
                    return (lambda kc: cx.xT[:, kc, 0:cx.N]), cx.xkeys, ev
                wsblock("H%d" % i, st_h)

                def st_c(cx, i=i):
                    def ev(j4, a_ap, a_key):
                        c = 4 * i + j4
                        set_halo(cx, j4, c, hu, "hu", st_mix)
                        hv = halo_view(cx, j4)
                        L, N = cx.L, cx.N
                        em.add("dve", lambda e: e.tensor_tensor(out=hv[:, :, 2:L + 2], in0=v3(cx, a_ap),
                                                                in1=v3(cx, cx.tB[:, j4, 0:N]), op=ALU.mult),
                               r=[a_key, k_(cx, "tB", j4)], w=[k_(cx, "tA", j4)])
                        save_halo(cx, j4, c, hu, "hu", cm_p, cm_s)
                        conv3(cx, j4, wcm, c)
                    return (lambda kc: cx.xT[:, kc, 0:cx.N]), cx.xkeys, ev
                wsblock("C%d" % i, st_c)
                if i == 0:
                    for f_ in P.pop("deferred", []):
                        f_()
                    for cx in ctxs:
                        for s in range(cx.NS):
                            if cx.prompt:
                                dma("sp", cx.xv[:, s, :], x_p[cx.row0 + s * 128: cx.row0 + (s + 1) * 128, :], r=[],
                                    w=[k_(cx, "xv", s, q) for q in range(NIB)], ring="io")
                            else:
                                dma("sp", cx.xv[0:NT, 0, :], x_s, r=[], w=[k_(cx, "xv", 0, q) for q in range(NIB)], ring="io")

                def st_b(cx, i=i):
                    def ev(j4, a_ap, a_key):
                        c = 4 * i + j4
                        N = cx.N
                        em.add("dve", lambda e: e.tensor_tensor(out=cx.yaT[:, c, 0:N], in0=a_ap, in1=cx.tB[:, j4, 0:N],
                                                                op=ALU.mult),
                               r=[a_key, k_(cx, "tB", j4)], w=[K_(cx.kya, c)])
                    return (lambda kc: cx.xT[:, kc, 0:cx.N]), cx.xkeys, ev
                wsblock("B%d" % i, st_b)
            ln_load(0)
            for i in range(NIB):
                def st_g(cx):
                    def ev(j4, a_ap, a_key):
                        N = cx.N
                        em.add("act", lambda e: e.activation(out=cx.tB[:, j4, 0:N], in_=a_ap, func=AF.Sigmoid), r=[a_key],
                               w=[k_(cx, "tB", j4)])
                    return (lambda kc: cx.xT[:, kc, 0:cx.N]), cx.xkeys, ev
                wsblock("GA%d" % i, st_g)

                def st_wa(cx):
                    def ev(j4, a_ap, a_key):
                        N = cx.N
                        em.add("dve", lambda e: e.tensor_tensor(out=cx.tC[:, j4, 0:N], in0=a_ap, in1=cx.tB[:, j4, 0:N],
                                                                op=ALU.mult),
                               r=[a_key, k_(cx, "tB", j4)], w=[k_(cx, "tC", j4)])
                    return (lambda kc: cx.yaT[:, kc, 0:cx.N]), [K_(cx.kya, c) for c in range(KC)], ev
                wsblock("WA%d" % i, st_wa)
                wsblock("GB%d" % i, st_g)

                def st_wb(cx, i=i):
                    def ev(j4, a_ap, a_key):
                        c = 4 * i + j4
                        N = cx.N
                        em.add("dve", lambda e: e.tensor_tensor(out=cx.tB[:, j4, 0:N], in0=a_ap, in1=cx.tB[:, j4, 0:N],
                                                                op=ALU.mult),
                               r=[a_key, k_(cx, "tB", j4)], w=[k_(cx, "tB", j4)])
                        em.add("pool", lambda e: e.tensor_tensor(out=cx.M[:, c, 0:N], in0=cx.tB[:, j4, 0:N],
                                                                 in1=cx.tC[:, j4, 0:N], op=ALU.add),
                               r=[k_(cx, "tB", j4), k_(cx, "tC", j4)], w=[K_(cx.kM, c)])
                    return (lambda kc: cx.ybv(kc)), cx.ybk, ev
                wsblock("WB%d" % i, st_wb)
            for n in range(NIB):
                wt, wkey = ws.get("WO%d" % n)
                for cx in ctxs:
                    RP = cx.RP
                    for s in range(cx.NS):
                        a = next_acc()
                        if not em.dry:
                            mm(acc[a][0:RP, :], [(cx.M[:, kc, s * 128: s * 128 + RP], wt[:, kc, :]) for kc in range(KC)],
                               r=[wkey] + cx.mkeys, w=[K_("acc", a)])
                        xs_ = cx.xv[0:RP, s, n * 512:(n + 1) * 512]
                        em.add("dve", lambda e, o=xs_, i=acc[a][0:RP, :]: e.scalar_tensor_tensor(
                            out=o, in0=o, scalar=ALPHA, in1=i, op0=ALU.mult, op1=ALU.add),
                            r=[K_("acc", a), k_(cx, "xv", s, n)], w=[k_(cx, "xv", s, n)])
                for cx in ctxs:
                    for s in range(cx.NS):
                        ln_stats(cx, s, n)

            G4 = min(4, KC)
            for cx in ctxs:
                RP = cx.RP
                for s, q in ln_finish(cx):
                    k4 = q * 4
                    t = next_tp()
                    for j in range(G4):
                        tr(tp[t][:, j * RP:(j + 1) * RP], cx.xv[0:RP, s, (k4 + j) * 128:(k4 + j + 1) * 128],
                           ident[0:RP, 0:RP], r=[k_(cx, "xv", s, q), K_("ident")], w=[K_("tp", t)])
                    copy_op(cx.M[:, k4:k4 + G4, s * 128: s * 128 + RP],
                            tp[t][:, 0:G4 * RP].rearrange("p (a b) -> p a b", a=G4),
                            r=[K_("tp", t)], w=[K_(cx.kM, c) for c in range(k4, k4 + G4)])
            if nxt is not None:
                load_xT(*nxt)
            for gi, (c0, ng) in enumerate(fgroups):
                if gi == 1 or len(fgroups) == 1:
                    ln_load(1)
                for i in range(c0 // 4, (c0 + ng) // 4):
                    def st_a(cx, i=i):
                        def ev(j4, a_ap, a_key):
                            c = 4 * i + j4
                            set_halo(cx, j4, c, ha, "ha", st_ffn)
                            hv = halo_view(cx, j4)
                            L, N = cx.L, cx.N
                            em.add("act", lambda e: e.activation(out=hv[:, :, 2:L + 2], in_=v3(cx, a_ap), func=AF.Copy),
                                   r=[a_key], w=[k_(cx, "tA", j4)])
                            save_halo(cx, j4, c, ha, "ha", cf_p, cf_s)
                            conv3(cx, j4, wcf, c)
                            em.add("act", lambda e: e.activation(out=cx.tB[:, j4, 0:N], in_=cx.tB[:, j4, 0:N], func=AF.Silu),
                                   r=[k_(cx, "tB", j4)], w=[k_(cx, "tB", j4)])
                        return (lambda kc: cx.M[:, kc, 0:cx.N]), cx.mkeys, ev
                    wsblock("UA%d" % i, st_a)

                    def st_bb(cx, i=i, gi=gi, c0=c0):
                        hT, hk, ko = cx.hT[gi % 2]

                        def ev(j4, a_ap, a_key):
                            c = 4 * i + j4
                            N = cx.N
                            o_ = hT[:, c - c0, 0:N]
                            em.add("dve", lambda e: e.tensor_tensor(out=o_, in0=a_ap, in1=cx.tB[:, j4, 0:N], op=ALU.mult),
                                   r=[a_key, k_(cx, "tB", j4)], w=[K_(hk, ko + c - c0)])
                        return (lambda kc: cx.M[:, kc, 0:cx.N]), cx.mkeys, ev
                    wsblock("UB%d" % i, st_bb)
                for n in range(NIB):
                    wt, wkey = ws.get("WD%d_%d" % (gi, n))
                    for cx in ctxs:
                        RP = cx.RP
                        hT, hk, ko = cx.hT[gi % 2]
                        hkeys = [K_(hk, ko + c) for c in range(ng)]
                        for s in range(cx.NS):
                            a = next_acc()
                            if not em.dry:
                                mm(acc[a][0:RP, :], [(hT[:, c, s * 128: s * 128 + RP], wt[:, c, :]) for c in range(ng)],
                                   r=[wkey] + hkeys, w=[K_("acc", a)])
                            xs_ = cx.xv[0:RP, s, n * 512:(n + 1) * 512]
                            xk = k_(cx, "xv", s, n)
                            if gi == 0:
                                em.add("dve", lambda e, o=xs_, i=acc[a][0:RP, :]: e.scalar_tensor_tensor(
                                    out=o, in0=o, scalar=ALPHA, in1=i, op0=ALU.mult, op1=ALU.add),
                                    r=[K_("acc", a), xk], w=[xk])
                            else:
                                em.add("dve", lambda e, o=xs_, i=acc[a][0:RP, :]: e.tensor_tensor(out=o, in0=o, in1=i,
                                                                                                  op=ALU.add),
                                       r=[K_("acc", a), xk], w=[xk])
                    if gi == len(fgroups) - 1:
                        for cx in ctxs:
                            for s in range(cx.NS):
                                ln_stats(cx, s, n)
            for cx in ctxs:
                if cx.prompt and cx.last_tile:
                    pk = [("ptC", j) for j in range(4)]
                    store_fm_table(hu, cm_p[cx.sq], 2, KC, lambda si: P["tC"][:, si, :], pk, lambda c: K_("hu", c))
                    store_fm_table(ha, cf_p[cx.sq], 2, FC, lambda si: P["tC"][:, si, :], pk, lambda c: K_("ha", c))
            for cx in ctxs:
                if not cx.prompt:
                    pk = [("ptC", j) for j in range(4)]
                    store_fm_table(smix.rearrange("p c s i -> p c (s i)"), cm_s.rearrange("s i d -> (s i) d"), NSQ * 2, KC,
                                   lambda si: P["tC"][:, si, :], pk, K_("stab"))
                    store_fm_table(sffn.rearrange("p c s i -> p c (s i)"), cf_s.rearrange("s i d -> (s i) d"), NSQ * 2, FC,
                                   lambda si: P["tC"][:, si, :], pk, K_("stab"))
            for cx in ctxs:
                for s, q in ln_finish(cx):
                    if q == nchk - 1:
                        rk_ = [k_(cx, "xv", s, qq) for qq in range(nchk)]
                        if cx.prompt:
                            def ystore(cx=cx, s=s, rk_=rk_):
                                dma("sp", y_p[cx.row0 + s * 128: cx.row0 + (s + 1) * 128, :], cx.xv[:, s, :], r=rk_, w=[],
                                    ring="io")
                            if cx.last_tile:
                                ystore()
                            else:
                                P.setdefault("deferred", []).append(ystore)
                        else:
                            dma("sp", y_s, cx.xv[0:NT, 0, :], r=rk_, w=[], ring="io")


        def schedule():
            stop = getattr(cfg, "stop", 99)
            prologue()
            if stop < 1:
                return
            phase0(0)
            prologue2()
            em.barrier()
            if stop < 2:
                return
            for sq in range(NP):
                if sq == NP - 1:
                    kv_shift()
                phase1(sq)
                em.barrier()
                if cfg.sample and sq == 0:
                    phase1s()
                    em.barrier()
                if stop < 3:
                    return
                P = phase2_setup()
                for tt in range(4):
                    ctxs = [make_ctx(P, True, sq, tt)]
                    if cfg.sample and sq == 0 and tt == 1:
                        ctxs.append(make_ctx(P, False))
                    phase2(P, ctxs, nxt=(sq, tt + 1) if tt < 3 else None)
                em.barrier()

        def fix_keys():
            pass

        em.dry = True
        ws.rec = True
        schedule()
        em.dry = False
        ws.rec = False
        accn[0] = 0
        tpn[0] = 0
        evn[0] = 0
        schedule()
        assert ws.i == len(ws.seq), (ws.i, len(ws.seq))
        print("ops:", len(em.ops), {e: sum(1 for o in em.ops if o.eng == e) for e in em.ENGS})
        em.plan()
        em.emit(nc, es)
    return nc


_CACHE = {}


def _run(cfg, inputs, ncores):
    key = (cfg.D, cfg.DF, cfg.NP, cfg.NSQ, cfg.FG, cfg.sample, getattr(cfg, "stop", 99), getattr(cfg, "p1stop", 99))
    if key not in _CACHE:
        _CACHE[key] = build_program(cfg)
    nc = _CACHE[key]
    consts = make_consts()
    f = lambda a: np.ascontiguousarray(np.asarray(a, dtype=np.float32))
    shared = {
        "w_in": f(inputs["w_in"]), "w_conv_mix": f(inputs["w_conv_mix"]), "w_branch_a": f(inputs["w_branch_a"]),
        "w_branch_b": f(inputs["w_branch_b"]), "w_out": f(inputs["w_out"]),
        "ln1_g": f(inputs["ln1_g"]).reshape(1, -1), "ln1_b": f(inputs["ln1_b"]).reshape(1, -1),
        "w_up": f(inputs["w_up"]), "w_conv_ffn": f(inputs["w_conv_ffn"]), "w_down": f(inputs["w_down"]),
        "ln2_g": f(inputs["ln2_g"]).reshape(1, -1), "ln2_b": f(inputs["ln2_b"]).reshape(1, -1),
    }
    shared.update(consts)
    NP, NSQ, D, DF = cfg.NP, cfg.NSQ, cfg.D, cfg.DF
    in_maps = []
    for c in range(ncores):
        m = dict(shared)
        m["x_p"] = f(inputs["x_prompt"][c * NP:(c + 1) * NP]).reshape(NP * S, D)
        m["x_s"] = f(inputs["x_sample"][c * NSQ:(c + 1) * NSQ]).reshape(NSQ * DS, D)
        m["st_mix"] = f(inputs["state_conv_mix"][c * NSQ:(c + 1) * NSQ]).reshape(NSQ * 2, D)
        m["st_ffn"] = f(inputs["state_conv_ffn"][c * NSQ:(c + 1) * NSQ]).reshape(NSQ * 2, DF)
        for g, cname in enumerate(("cache_kv0", "cache_kv1", "cache_kv2")):
            m["c%d" % g] = f(inputs[cname][c * NSQ:(c + 1) * NSQ]).reshape(NSQ, GROUPS[g][0], 1024)
        in_maps.append(m)
    res = run_bass_kernel_spmd(nc, in_maps, core_ids=list(range(ncores)))
    R = res.results
    cat = lambda name: np.concatenate([np.asarray(r[name], dtype=np.float32) for r in R], axis=0)
    B = NP * ncores
    BS = NSQ * ncores
    outs = (
        cat("y_p").reshape(B, S, D),
        cat("y_s").reshape(BS, DS, D),
        cat("cm_p").reshape(B, 2, D),
        cat("kv0_p").reshape(B, 128, 2, 4, 128),
        cat("kv1_p").reshape(B, 512, 2, 4, 128),
        cat("kv2_p").reshape(B, 2048, 2, 4, 128),
        cat("cf_p").reshape(B, 2, DF),
        cat("cm_s").reshape(BS, 2, D),
        cat("kv0_s").reshape(BS, 128, 2, 4, 128),
        cat("kv1_s").reshape(BS, 512, 2, 4, 128),
        cat("kv2_s").reshape(BS, 2048, 2, 4, 128),
        cat("cf_s").reshape(BS, 2, DF),
    )
    return outs


def kernel(**inputs):
    cfg = Cfg()
    return _run(cfg, inputs, NCORES)
```

```python
import numpy as np
from contextlib import ExitStack
import concourse.bass as bass
import concourse.mybir as mybir
from concourse.bass_utils import run_bass_kernel_spmd

F32 = mybir.dt.float32
BF16 = mybir.dt.bfloat16
U8 = mybir.dt.uint8
ALU = mybir.AluOpType
AF = mybir.ActivationFunctionType

S = 2048
HD = 128
HPG = 4
NG = 3
GROUPS = ((128, 1), (512, 4), (2048, 16))
ATT = 1536
SCALE = HD ** -0.5
NEG = -10000.0
EPS = 1e-5
ALPHA = 2.0 ** 0.25
DS = 4
NCORES = 8


class Cfg:
    def __init__(self, D=2048, DF=5632, NP=2, NSQ=4, FG=8, sample=True):
        self.D, self.DF, self.NP, self.NSQ, self.FG, self.sample = D, DF, NP, NSQ, FG, sample
        self.KC = D // 128
        self.FC = DF // 128
        self.E = 5 * D + 3 * ATT
        self.oH, self.oBG, self.oCG = 0, D, 2 * D
        self.oQ, self.oK, self.oV = 3 * D, 3 * D + ATT, 3 * D + 2 * ATT
        self.oGA, self.oGB = 3 * D + 3 * ATT, 4 * D + 3 * ATT
        self.NT = 16
        assert NSQ * DS == self.NT


class Op:
    __slots__ = ("eng", "fn", "deps", "dma", "inc", "sem", "val", "waits")

    def __init__(self, eng, fn, deps, dma):
        self.eng, self.fn, self.deps, self.dma = eng, fn, deps, dma
        self.inc = False
        self.sem = None
        self.val = 0
        self.waits = None


class Em:
    ENGS = ("pe", "act", "dve", "pool", "sp")
    RING = 8

    def __init__(self):
        self.ops = []
        self.lastw = {}
        self.rd = {}
        self.dry = False
        self.last_on = {}
        self.dma_all = []
        self.lookahead = True

    def add(self, eng, fn, r=(), w=(), dma=None):
        if self.dry:
            return
        deps = set()
        for k in r:
            x = self.lastw.get(k)
            if x is not None:
                deps.add(x)
        for k in w:
            x = self.lastw.get(k)
            if x is not None:
                deps.add(x)
            for y in self.rd.get(k, ()):
                deps.add(y)
        i = len(self.ops)
        self.ops.append(Op(eng, fn, deps, dma))
        for k in w:
            self.lastw[k] = i
            self.rd[k] = []
        for k in r:
            lst = self.rd.setdefault(k, [])
            if dma is None:
                lst[:] = [y for y in lst if not (self.ops[y].dma is None and self.ops[y].eng == eng)]
            lst.append(i)
        if dma is None:
            self.last_on[eng] = i
        else:
            self.dma_all.append(i)

    def barrier(self):
        if self.dry:
            return
        deps = set(self.last_on.values()) | set(i for i in self.dma_all if self.ops[i].dma not in ("pro", "kv"))
        for e in self.ENGS:
            self.ops.append(Op(e, None, set(deps), None))
        keep = {k: v for k, v in self.lastw.items() if k[0] == "wscr"}
        self.lastw.clear()
        self.lastw.update(keep)
        self.rd.clear()
        self.dma_all = [i for i in self.dma_all if self.ops[i].dma in ("pro", "kv")]
        for e in self.ENGS:
            self.last_on[e] = len(self.ops) - len(self.ENGS) + self.ENGS.index(e)

    def plan(self):
        ops = self.ops
        ring_hist = {}
        for i, o in enumerate(ops):
            if o.dma is not None:
                h = ring_hist.setdefault(o.dma, [])
                if len(h) >= self.RING:
                    o.deps.add(h[len(h) - self.RING])
                h.append(i)
        self.ring_hist = ring_hist
        for i, o in enumerate(ops):
            for d in o.deps:
                y = ops[d]
                if y.dma is None and not (y.eng == "pe" and o.eng == "pe" and o.dma is None):
                    y.inc = True
        cnt = {e: 0 for e in self.ENGS}
        ringcnt = {}
        for i, o in enumerate(ops):
            if o.dma is not None:
                k = ringcnt.get(o.dma, 0)
                ringcnt[o.dma] = k + 1
                o.sem = ("ring", o.dma, k % self.RING)
                o.val = 16 * (k // self.RING + 1)
            elif o.inc and o.fn is not None:
                cnt[o.eng] += 1
                o.sem = ("eng", o.eng)
                o.val = cnt[o.eng]
            else:
                o.sem = ("eng", o.eng)
                o.val = cnt[o.eng]
        waited = {e: {} for e in self.ENGS}

        def needs_of(i, before=None):
            o = ops[i]
            need = {}
            for d in o.deps:
                if before is not None and d >= before:
                    continue
                y = ops[d]
                if y.dma is None:
                    if y.eng == "pe" and o.eng == "pe" and o.dma is None:
                        continue
                    if y.fn is None and y.eng == o.eng:
                        continue
                    if y.val == 0:
                        continue
                if y.sem not in need or need[y.sem] < y.val:
                    need[y.sem] = y.val
            return need

        pe_list = [i for i, o in enumerate(ops) if o.eng == "pe" and o.dma is None and o.fn is not None]
        nxt_pe = {pe_list[k]: pe_list[k + 1] for k in range(len(pe_list) - 1)}
        for i, o in enumerate(ops):
            need = needs_of(i)
            if self.lookahead and i in nxt_pe:
                for s_, v_ in needs_of(nxt_pe[i], before=i).items():
                    if s_[0] == "ring":
                        continue
                    if s_ not in need or need[s_] < v_:
                        need[s_] = v_
            wl = []
            wd = waited[o.eng]
            for s, v in need.items():
                if wd.get(s, 0) >= v:
                    continue
                wd[s] = v
                wl.append((s, v))
            o.waits = wl
        self.final = []
        for name, h in ring_hist.items():
            n = len(h)
            for slot in range(min(n, self.RING)):
                k_last = ((n - 1 - slot) // self.RING) * self.RING + slot
                self.final.append((("ring", name, slot), 16 * (k_last // self.RING + 1)))
        for e in ("pe", "act", "dve", "pool"):
            if cnt[e]:
                self.final.append((("eng", e), cnt[e]))
        self.cnt = cnt
        print("waits per engine:", {e: sum(len(o.waits) for o in ops if o.eng == e) for e in self.ENGS})

    def emit(self, nc, es):
        sems = {}

        def sem(key):
            if key not in sems:
                sems[key] = es.enter_context(nc.semaphore("s_" + "_".join(str(x) for x in key)))
            return sems[key]

        for o in self.ops:
            sem(o.sem)
            for s, _ in o.waits:
                sem(s)
        for s, _ in self.final:
            sem(s)
        per = {e: [o for o in self.ops if o.eng == e] for e in self.ENGS}
        final = self.final

        def run(e, eng_name):
            for o in per[eng_name]:
                for s, v in o.waits:
                    e.wait_ge(sems[s], v)
                if o.fn is None:
                    continue
                ins = o.fn(e)
                if o.dma is not None:
                    ins.then_inc(sems[o.sem], 16)
                elif o.inc:
                    ins.then_inc(sems[o.sem], 1)
            if eng_name == "sp":
                for s, v in final:
                    e.wait_ge(sems[s], v)

        with nc.Block() as block:
            @block.tensor
            def _(e):
                run(e, "pe")

            @block.scalar
            def _(e):
                run(e, "act")

            @block.vector
            def _(e):
                run(e, "dve")

            @block.gpsimd
            def _(e):
                run(e, "pool")

            @block.sync
            def _(e):
                run(e, "sp")


def _slopes():
    h = np.arange(1, 13, dtype=np.float64)
    return np.exp2(-8.0 * h / 12.0).reshape(3, 4)


def make_consts():
    sl = _slopes()
    k = np.arange(128)[:, None].astype(np.float64)
    q = np.arange(128)[None, :].astype(np.float64)
    biasT = np.zeros((4, 128, 3, 256), np.float32)
    for g, (_, dil) in enumerate(GROUPS):
        for j in range(4):
            c = sl[g, j] * dil
            du0 = q - k
            t0 = np.where(du0 >= 0, -c * du0, NEG)
            du1 = 128 + q - k
            t1 = np.where(du1 <= 128, -c * du1, NEG)
            biasT[j, :, g, 0:128] = t0
            biasT[j, :, g, 128:256] = t1
    sbias = np.zeros((128, 3, 4, 4), np.float32)
    r = np.arange(128).astype(np.float64)
    for g, (_, dil) in enumerate(GROUPS):
        for h in range(4):
            c = sl[g, h] * dil
            for t in range(4):
                if g == 0:
                    sbias[:, g, h, t] = np.where(r >= t, -c * (128 + t - r), NEG)
                else:
                    sbias[:, g, h, t] = -c * (128 - r)
    nbias = np.full((16, 3, 4, 16), NEG, np.float32)
    for g, (_, dil) in enumerate(GROUPS):
        for h in range(4):
            c = sl[g, h] * dil
            for b in range(4):
                for s in range(4):
                    for t in range(4):
                        ok = (s <= t) if g == 0 else (s == t)
                        if ok:
                            nbias[4 * b + s, g, h, 4 * b + t] = -c * (t - s)
    ident = np.eye(128, dtype=np.float32)
    return dict(c_ident=ident, c_biasT=biasT.reshape(4, 128, 768),
                c_sbias=sbias.reshape(128, 48), c_nbias=nbias.reshape(16, 192))


def build_program(cfg):
    D, DF, KC, FC, NP, NSQ, NT = cfg.D, cfg.DF, cfg.KC, cfg.FC, cfg.NP, cfg.NSQ, cfg.NT
    nc = bass.Bass("TRN2", target_bir_lowering=False)

    def din(name, shape, dt=F32):
        return nc.dram_tensor(name, list(shape), dt, kind="ExternalInput").ap()

    def dout(name, shape, dt=F32):
        return nc.dram_tensor(name, list(shape), dt, kind="ExternalOutput").ap()

    x_p = din("x_p", [NP * S, D])
    x_s = din("x_s", [NT, D])
    st_mix = din("st_mix", [NSQ * 2, D])
    st_ffn = din("st_ffn", [NSQ * 2, DF])
    caches = [din("c%d" % g, [NSQ, GROUPS[g][0], 1024]) for g in range(3)]
    w_in = din("w_in", [D, cfg.E])
    w_cm = din("w_conv_mix", [3, D])
    w_a = din("w_branch_a", [D, D])
    w_b = din("w_branch_b", [512, D])
    w_o = din("w_out", [D, D])
    ln1g, ln1b = din("ln1_g", [1, D]), din("ln1_b", [1, D])
    w_up = din("w_up", [D, 2 * DF])
    w_cf = din("w_conv_ffn", [3, DF])
    w_dn = din("w_down", [DF, D])
    ln2g, ln2b = din("ln2_g", [1, D]), din("ln2_b", [1, D])
    c_ident = din("c_ident", [128, 128])
    c_biasT = din("c_biasT", [4, 128, 768])
    c_sbias = din("c_sbias", [128, 48])
    c_nbias = din("c_nbias", [16, 192])

    y_p = dout("y_p", [NP * S, D])
    y_s = dout("y_s", [NT, D])
    cm_p = dout("cm_p", [NP, 2, D])
    kv_p = [dout("kv%d_p" % g, [NP, GROUPS[g][0], 2, 4, 128]) for g in range(3)]
    cf_p = dout("cf_p", [NP, 2, DF])
    cm_s = dout("cm_s", [NSQ, 2, D])
    kv_s = [dout("kv%d_s" % g, [NSQ, GROUPS[g][0], 1024]) for g in range(3)]
    cf_s = dout("cf_s", [NSQ, 2, DF])

    blocks = {}
    order = []

    def defblock(name, pieces, kcn):
        ncols = sum(p.shape[1] for p in pieces)
        blocks[name] = (len(order), kcn, ncols, pieces)
        order.append(name)

    for j in range(4):
        for kind, off in (("Q", cfg.oQ), ("K", cfg.oK), ("V", cfg.oV)):
            defblock("%s%d" % (kind, j),
                     [w_in[:, off + g * 512 + j * 128: off + g * 512 + (j + 1) * 128] for g in range(3)], KC)
    NIB = D // 512
    for i in range(NIB):
        defblock("H%d" % i, [w_in[:, cfg.oH + 512 * i: cfg.oH + 512 * (i + 1)]], KC)
        defblock("C%d" % i, [w_in[:, cfg.oCG + 512 * i: cfg.oCG + 512 * (i + 1)]], KC)
        defblock("B%d" % i, [w_in[:, cfg.oBG + 512 * i: cfg.oBG + 512 * (i + 1)]], KC)
    for i in range(NIB):
        defblock("GA%d" % i, [w_in[:, cfg.oGA + 512 * i: cfg.oGA + 512 * (i + 1)]], KC)
        defblock("WA%d" % i, [w_a[:, 512 * i: 512 * (i + 1)]], KC)
        defblock("GB%d" % i, [w_in[:, cfg.oGB + 512 * i: cfg.oGB + 512 * (i + 1)]], KC)
        defblock("WB%d" % i, [w_b[:, 512 * i: 512 * (i + 1)]], 4)
    for n in range(NIB):
        defblock("WO%d" % n, [w_o[:, 512 * n: 512 * (n + 1)]], KC)
    NFB = DF // 512
    for i in range(NFB):
        defblock("UA%d" % i, [w_up[:, 512 * i: 512 * (i + 1)]], KC)
        defblock("UB%d" % i, [w_up[:, DF + 512 * i: DF + 512 * (i + 1)]], KC)
    fgroups = []
    c0 = 0
    while c0 < FC:
        ng = min(cfg.FG, FC - c0)
        fgroups.append((c0, ng))
        c0 += ng
    for gi, (c0, ng) in enumerate(fgroups):
        for n in range(NIB):
            defblock("WD%d_%d" % (gi, n), [w_dn[c0 * 128:(c0 + ng) * 128, 512 * n: 512 * (n + 1)]], ng)
    NBLK = len(order)
    wscr = nc.dram_tensor("wscr", [NBLK, 128, 16 * 512], BF16, kind="Internal").ap()
    xTs = nc.dram_tensor("xTs", [NP * 4, 128, KC * 512], BF16, kind="Internal").ap()
    xTsm = nc.dram_tensor("xTsm", [128, KC * NT], BF16, kind="Internal").ap()

    em = Em()

    with ExitStack() as es:
        NBYTES = 212736
        big = es.enter_context(nc.sbuf_tensor("big", [128, NBYTES], U8))
        acc = [es.enter_context(nc.psum_tensor("acc%d" % i, [128, 512], F32)) for i in range(4)]
        tp = [es.enter_context(nc.psum_tensor("tp%d" % i, [128, 512], F32)) for i in range(2)]
        sTb = [es.enter_context(nc.psum_tensor("sT%d" % i, [128, 512], F32)) for i in range(2)]

        class Alloc:
            def __init__(self, base, limit):
                self.p, self.limit = base, limit

            def get(self, shape, dt, parts=128):
                esz = 2 if dt == BF16 else 4
                n = int(np.prod(shape)) * esz
                n = (n + 63) // 64 * 64
                off = self.p
                self.p += n
                assert self.p <= self.limit, ("SBUF overflow", self.p, self.limit)
                ap = big[0:parts, off:off + int(np.prod(shape)) * esz].bitcast(dt)
                if len(shape) == 2:
                    ap = ap.rearrange("p (a b) -> p a b", a=shape[0])
                elif len(shape) == 3:
                    ap = ap.rearrange("p (a b c) -> p a b c", a=shape[0], b=shape[1])
                return ap

        al = Alloc(0, NBYTES)
        ident = al.get([128], F32)
        identb = al.get([128], BF16)
        onesb = al.get([128], BF16)
        onesf = al.get([128], F32)
        wcm = al.get([KC, 3], F32)
        wcf = al.get([FC, 3], F32)
        hu = al.get([KC, 2], F32)
        ha = al.get([FC, 2], F32)
        smix = al.get([KC, NSQ, 2], F32)
        sffn = al.get([FC, NSQ, 2], F32)
        wring = [al.get([16, 512], BF16) for _ in range(3)]
        bufA = al.get([16, 512], BF16)
        bufB = al.get([16, 512], BF16)
        ybT = al.get([4, S], BF16)
        ybTs = al.get([4, NT], BF16)
        phase_base = al.p

        def K_(name, *idx):
            return (name,) + idx

        accn = [0]
        tpn = [0]

        def next_acc():
            i = accn[0] % 4
            accn[0] += 1
            return i

        def next_acc2():
            i = accn[0] % 2
            accn[0] += 1
            return i

        def next_tp():
            i = tpn[0] % 2
            tpn[0] += 1
            return i

        evn = [0]

        def copy_op(out, in_, r, w, eng=None):
            if out.dtype == F32 and eng != "pool":
                eng = "act"
            if eng is None:
                eng = "act" if evn[0] % 2 == 0 else "dve"
                evn[0] += 1
            if eng == "act":
                em.add("act", lambda e: e.activation(out=out, in_=in_, func=AF.Copy), r=r, w=w)
            elif eng == "dve":
                em.add("dve", lambda e: e.tensor_copy(out=out, in_=in_), r=r, w=w)
            else:
                em.add("pool", lambda e: e.tensor_copy(out=out, in_=in_), r=r, w=w)

        def mm(out, pairs, r, w, skip=False, first_start=True):
            def fn(e):
                n = len(pairs)
                ins = None
                for i, (a, b) in enumerate(pairs):
                    if skip:
                        ins = e.matmul(out, lhsT=a, rhs=b, start=(first_start and i == 0), stop=(i == n - 1),
                                       skip_group_check=True)
                    else:
                        ins = e.matmul(out, lhsT=a, rhs=b, start=(i == 0), stop=(i == n - 1))
                return ins
            em.add("pe", fn, r=r, w=w)

        def tr(out, in_, idn, r, w):
            em.add("pe", lambda e: e.transpose(out, in_, idn), r=r, w=w)

        def dma(eng, out, in_, r, w, ring, slow=False, **kw):
            if slow:
                em.add(eng, lambda e: e.dma_start(out=out, in_=in_, allow_slow_non_contiguous=True, **kw),
                       r=r, w=w, dma=ring)
            else:
                em.add(eng, lambda e: e.dma_start(out=out, in_=in_, **kw), r=r, w=w, dma=ring)

        stgk = [0]

        def load_fm_table(tab, src, R, nchk, stages, key):
            HC = 16
            for h0 in range(0, nchk, HC):
                hn = min(HC, nchk - h0)
                si = stgk[0] % len(stages)
                stgk[0] += 1
                sb = stages[si]
                sk = K_("fmstage", si)
                dma("sp", sb[0:R, 0:hn * 128], src[:, h0 * 128:(h0 + hn) * 128], r=[], w=[sk], ring="io")
                for c4 in range(0, hn, 4):
                    n = min(4, hn - c4)
                    t = next_tp()
                    for j in range(n):
                        tr(tp[t][:, j * R:(j + 1) * R], sb[0:R, (c4 + j) * 128:(c4 + j + 1) * 128], ident[0:R, 0:R],
                           r=[sk, K_("ident")], w=[K_("tp", t)])
                    copy_op(tab[:, h0 + c4:h0 + c4 + n, :], tp[t][:, 0:n * R].rearrange("p (a b) -> p a b", a=n),
                            r=[K_("tp", t)], w=[key])

        def store_fm_table(tab, dst, R, nchk, stage_of, stage_keys, key):
            for c4 in range(0, nchk, 4):
                n = min(4, nchk - c4)
                t = next_tp()
                for j in range(n):
                    kk_ = key(c4 + j) if callable(key) else key
                    tr(tp[t][0:R, j * 128:(j + 1) * 128], tab[:, c4 + j, :], ident, r=[kk_, K_("ident")], w=[K_("tp", t)])
                si = (c4 // 4) % len(stage_keys)
                sb = stage_of(si)
                copy_op(sb[0:R, 0:n * 128], tp[t][0:R, 0:n * 128], r=[K_("tp", t)], w=[stage_keys[si]])
                dma("sp", dst[:, c4 * 128:(c4 + n) * 128], sb[0:R, 0:n * 128], r=[stage_keys[si]], w=[], ring="io")

        class WS:
            def __init__(self):
                self.seq = []
                self.rec = True
                self.i = 0
                self.loaded = 0
                self.PF = 2

            def _load(self, k):
                name = self.seq[k]
                idx, kcn, ncols, _ = blocks[name]
                slot = k % 3
                dst = wring[slot]
                dma("sp", dst.rearrange("p a b -> p (a b)")[:, 0:kcn * ncols],
                    wscr[idx][:, 0:kcn * ncols],
                    r=[K_("wscr", idx)], w=[K_("w", slot)], ring="w")

            def prefetch(self, n):
                if self.rec:
                    return
                while self.loaded <= min(self.i - 1 + n, len(self.seq) - 1):
                    self._load(self.loaded)
                    self.loaded += 1

            def get(self, name, pf=None):
                if self.rec:
                    self.seq.append(name)
                    return None, None
                k = self.i
                assert self.seq[k] == name
                self.i += 1
                pf = self.PF if pf is None else pf
                while self.loaded <= min(k + pf, len(self.seq) - 1):
                    self._load(self.loaded)
                    self.loaded += 1
                slot = k % 3
                kcn, ncols = blocks[name][1], blocks[name][2]
                return wring[slot].rearrange("p a b -> p (a b)")[:, 0:kcn * ncols].rearrange(
                    "p (a b) -> p a b", a=kcn), K_("w", slot)

        ws = WS()

        def prologue():
            dma("sp", ident, c_ident, r=[], w=[K_("ident")], ring="io")
            em.add("dve", lambda e: e.tensor_copy(out=identb, in_=ident), r=[K_("ident")], w=[K_("identb")])
            em.add("pool", lambda e: e.memset(onesb, 1.0), r=[], w=[K_("onesb")])
            em.add("pool", lambda e: e.memset(onesf, 1.0), r=[], w=[K_("onesf")])
            for name in order:
                idx, kcn, ncols, pieces = blocks[name]
                dstv = wscr[idx][:, 0:kcn * ncols].rearrange("p (a b) -> p a b", a=kcn)
                c = 0
                for pc in pieces:
                    wdt = pc.shape[1]
                    dma("pool", dstv[:, :, c:c + wdt], pc.rearrange("(kc p) n -> p kc n", p=128),
                        r=[], w=[K_("wscr", idx)], ring="pro")
                    c += wdt
        def prologue2():
            pass

        kvn = [0]

        def kv_shift():
            if cfg.sample:
                for g in range(3):
                    L = GROUPS[g][0]
                    n = L - DS
                    step = 512
                    for b in range(NSQ):
                        for r0 in range(0, n, step):
                            r1 = min(n, r0 + step)
                            dma("pool", kv_s[g][b, r0:r1, :], caches[g][b, DS + r0:DS + r1, :], r=[],
                                w=[K_("kvchain", kvn[0] % 2)], ring="kv")
                            kvn[0] += 1

        def phase0(sq0):
            al0 = Alloc(phase_base, NBYTES)
            xs = [al0.get([D], F32) for _ in range(2)]
            xsn = 0
            G4 = min(4, KC)
            for sq in range(sq0, sq0 + 1):
                for tt in range(4):
                    buf = bufA if (sq * 4 + tt) % 2 == 0 else bufB
                    bk = "A" if (sq * 4 + tt) % 2 == 0 else "B"
                    for s in range(4):
                        xb = xs[xsn % 2]
                        xk = K_("xs", xsn % 2)
                        xsn += 1
                        r0 = sq * S + tt * 512 + s * 128
                        dma("sp", xb, x_p[r0:r0 + 128, :], r=[], w=[xk], ring="io")
                        for k4 in range(0, KC, G4):
                            t = next_tp()
                            for j in range(G4):
                                tr(tp[t][:, j * 128:(j + 1) * 128], xb[:, (k4 + j) * 128:(k4 + j + 1) * 128], ident,
                                   r=[xk, K_("ident")], w=[K_("tp", t)])
                            copy_op(buf[:, k4:k4 + G4, s * 128:(s + 1) * 128],
                                    tp[t][:, 0:G4 * 128].rearrange("p (a b) -> p a b", a=G4),
                                    r=[K_("tp", t)], w=[K_(bk, "all")])
                    dma("sp", xTs[sq * 4 + tt][:, 0:KC * 512].rearrange("p (a b) -> p a b", a=KC), buf[:, 0:KC, :],
                        r=[K_(bk, "all")], w=[K_("xTs", sq * 4 + tt)], ring="io")
            if sq0 != 0:
                return
            stg0 = [al0.get([2048], F32) for _ in range(2)]
            load_fm_table(wcm, w_cm, 3, KC, stg0, K_("wctab"))
            load_fm_table(wcf, w_cf, 3, FC, stg0, K_("wctab"))
            if cfg.sample:
                xb = xs[xsn % 2]
                xk = K_("xs", xsn % 2)
                dma("sp", xb[0:NT, :], x_s, r=[], w=[xk], ring="io")
                for k4 in range(0, KC, G4):
                    t = next_tp()
                    for j in range(G4):
                        tr(tp[t][:, j * NT:(j + 1) * NT], xb[0:NT, (k4 + j) * 128:(k4 + j + 1) * 128], ident[0:NT, 0:NT],
                           r=[xk, K_("ident")], w=[K_("tp", t)])
                    copy_op(bufA[:, k4:k4 + G4, 0:NT], tp[t][:, 0:G4 * NT].rearrange("p (a b) -> p a b", a=G4),
                            r=[K_("tp", t)], w=[K_("A", "all")])
                dma("sp", xTsm.rearrange("p (a b) -> p a b", a=KC), bufA[:, 0:KC, 0:NT],
                    r=[K_("A", "all")], w=[K_("xTsm")], ring="io")

        def phase1(sq):
            al1 = Alloc(phase_base, NBYTES)
            qT = al1.get([3, S], BF16)
            kT = al1.get([3, S], BF16)
            vT = al1.get([3, S], BF16)
            kf = al1.get([3, 512], F32)
            vf = al1.get([3, 512], F32)
            vB = al1.get([3, 16, 128], BF16)
            an = al1.get([S], F32)
            ad = al1.get([S], F32)
            pT = [al1.get([256], BF16) for _ in range(6)]
            stmp = [al1.get([256], F32) for _ in range(2)]
            stg = [al1.get([4, 128], F32) for _ in range(3)]
            bias3 = al1.get([3, 256], F32)
            xs1 = [al1.get([D], F32) for _ in range(2)]
            xs1n = [0]
            pend_x = []
            stgn = [0]
            xtn = [0]
            p1s = getattr(cfg, "p1stop", 99)
            for j in range(4):
                dma("sp", bias3.rearrange("p a b -> p (a b)"), c_biasT[j], r=[], w=[K_("bias3")], ring="io")
                em.add("dve", lambda e: e.memset(an, 0.0), r=[], w=[K_("an")])
                em.add("dve", lambda e: e.memset(ad, 0.0), r=[], w=[K_("ad")])
                wq, kq = ws.get("Q%d" % j, pf=0)
                wk, kk = ws.get("K%d" % j, pf=0)
                wv, kv = ws.get("V%d" % j, pf=0)
                for tt in range(4):
                    xb = bufA if xtn[0] % 2 == 0 else bufB
                    xk = K_("A" if xtn[0] % 2 == 0 else "B", "all")
                    xtn[0] += 1
                    if j == 0 and sq == 0 and tt == 0:
                        dma("sp", xb[:, 0:KC, :], xTs[sq * 4 + tt][:, 0:KC * 512].rearrange("p (a b) -> p a b", a=KC),
                            r=[K_("xTs", sq * 4 + tt)], w=[xk], ring="io")
                    elif j == 0 and sq > 0:
                        G4 = min(4, KC)
                        for s_ in range(4):
                            if pend_x:
                                xsb, xsk = pend_x.pop(0)
                            else:
                                xsb = xs1[xs1n[0] % 2]
                                xsk = K_("xs1", xs1n[0] % 2)
                                xs1n[0] += 1
                                r0 = sq * S + tt * 512 + s_ * 128
                                dma("sp", xsb, x_p[r0:r0 + 128, :], r=[], w=[xsk], ring="io")
                            for k4 in range(0, KC, G4):
                                t = next_tp()
                                for jj in range(G4):
                                    tr(tp[t][:, jj * 128:(jj + 1) * 128], xsb[:, (k4 + jj) * 128:(k4 + jj + 1) * 128], ident,
                                       r=[xsk, K_("ident")], w=[K_("tp", t)])
                                copy_op(xb[:, k4:k4 + G4, s_ * 128:(s_ + 1) * 128],
                                        tp[t][:, 0:G4 * 128].rearrange("p (a b) -> p a b", a=G4),
                                        r=[K_("tp", t)], w=[xk])
                        dma("sp", xTs[sq * 4 + tt][:, 0:KC * 512].rearrange("p (a b) -> p a b", a=KC), xb[:, 0:KC, :],
                            r=[xk], w=[K_("xTs", sq * 4 + tt)], ring="io")
                        if tt < 3:
                            for s2_ in range(2):
                                nb_ = xs1[xs1n[0] % 2]
                                nk_ = K_("xs1", xs1n[0] % 2)
                                xs1n[0] += 1
                                r0 = sq * S + (tt + 1) * 512 + s2_ * 128
                                dma("sp", nb_, x_p[r0:r0 + 128, :], r=[], w=[nk_], ring="io")
                                pend_x.append((nb_, nk_))
                    nj, ntt = (j, tt + 1) if tt < 3 else (j + 1, 0)
                    if (nj >= 1 or sq == 0) and nj < 4:
                        nxb = bufA if xtn[0] % 2 == 0 else bufB
                        nxk = K_("A" if xtn[0] % 2 == 0 else "B", "all")
                        dma("sp", nxb[:, 0:KC, :], xTs[sq * 4 + ntt][:, 0:KC * 512].rearrange("p (a b) -> p a b", a=KC),
                            r=[K_("xTs", sq * 4 + ntt)], w=[nxk], ring="io")
                    for kind, wt, wkey, dst, fdst in (("Q", wq, kq, qT, None), ("K", wk, kk, kT, kf), ("V", wv, kv, vT, vf)):
                        if p1s < 1:
                            continue
                        import os as _os
                        if _os.environ.get("P1KIND") and kind not in _os.environ.get("P1KIND"):
                            continue
                        for c in range(3):
                            a = next_acc()
                            if not em.dry:
                                mm(acc[a][:, 0:512],
                                   [(wt[:, kc, c * 128:(c + 1) * 128], xb[:, kc, :]) for kc in range(KC)],
                                   r=[wkey, xk], w=[K_("acc", a)])
                            copy_op(dst[:, c, tt * 512:(tt + 1) * 512], acc[a][:, 0:512],
                                    r=[K_("acc", a)], w=[K_(kind + "T", c, tt)], eng="act")
                            if fdst is not None:
                                win = GROUPS[c][0]
                                if (tt + 1) * 512 > S - win:
                                    copy_op(fdst[:, c, :], acc[a][:, 0:512], r=[K_("acc", a)], w=[K_(kind + "f", c)],
                                            eng="act")
                        if fdst is not None and p1s >= 2:
                            which = 0 if kind == "K" else 1
                            for g in range(3):
                                win = GROUPS[g][0]
                                ms = [m for m in range(4) if tt * 512 + m * 128 >= S - win]
                                if not ms:
                                    continue
                                t = next_tp()
                                for m in ms:
                                    tr(tp[t][:, m * 128:(m + 1) * 128], fdst[:, g, m * 128:(m + 1) * 128], ident,
                                       r=[K_(kind + "f", g), K_("ident")], w=[K_("tp", t)])
                                sg = stgn[0] % 3
                                stgn[0] += 1
                                m0, nm = ms[0], len(ms)
                                copy_op(stg[sg][:, m0:m0 + nm, :],
                                        tp[t][:, m0 * 128:(m0 + nm) * 128].rearrange("p (a b) -> p a b", a=nm),
                                        r=[K_("tp", t)], w=[K_("stg", sg)])
                                row0 = tt * 512 + m0 * 128 - (S - win)
                                dma("sp", kv_p[g][sq, row0:row0 + nm * 128, which, j, :].rearrange("(m p) d -> p m d", p=128),
                                    stg[sg][:, m0:m0 + nm, :], r=[K_("stg", sg)], w=[], ring="io")
                ws.prefetch(3)
                if p1s < 3:
                    continue
                for g, (win, dil) in enumerate(GROUPS):
                    nblk = 16 // dil
                    for k4 in range(0, 16, 4):
                        t = next_tp()
                        tpb = tp[t][:, :].bitcast(BF16)
                        for i4 in range(4):
                            kbi = k4 + i4
                            rcl, b = kbi // nblk, kbi % nblk
                            st = rcl + dil * 128 * b
                            tr(tpb[:, i4 * 128:(i4 + 1) * 128], vT[:, g, st:st + dil * 127 + 1:dil], identb,
                               r=[K_("VT", g, tt_) for tt_ in range(4)] + [K_("identb")], w=[K_("tp", t)])
                        copy_op(vB[:, g, k4:k4 + 4, :], tpb[:, 0:512].rearrange("p (a b) -> p a b", a=4),
                                r=[K_("tp", t)], w=[K_("vB", g, k4 // 4)])
                if p1s < 4:
                    continue
                items = []
                for g, (win, dil) in enumerate(GROUPS):
                    nblk = 16 // dil
                    for rcl in range(dil):
                        for b in range(nblk):
                            items.append((g, dil, nblk, rcl, b))
                NPT = 6
                LA = 2
                state = {"bank": None, "pend": [], "prev": None, "bkn": 0}

                def emit_S(idx):
                    g, dil, nblk, rcl, b = items[idx]
                    ncols = 256 if b < nblk - 1 else 128
                    st = rcl + dil * 128 * b
                    hs = idx % 2
                    sTh = sTb[hs][:, 0:ncols]
                    qkeys = [K_("QT", g, t_) for t_ in range(4)]
                    kkeys = [K_("KT", g, t_) for t_ in range(4)]
                    mm(sTh, [(kT[:, g, st:st + dil * 127 + 1:dil], qT[:, g, st:st + dil * (ncols - 1) + 1:dil])],
                       r=qkeys + kkeys, w=[K_("sT", hs)])
                    sm = stmp[hs]
                    smk = K_("stmp", hs)
                    em.add("dve", lambda e, o=sm[:, 0:ncols], i=sTh, bb=bias3[:, g, 0:ncols]:
                           e.scalar_tensor_tensor(out=o, in0=i, scalar=SCALE, in1=bb, op0=ALU.mult, op1=ALU.add),
                           r=[K_("sT", hs), K_("bias3")], w=[smk])
                    ps = idx % NPT
                    em.add("act", lambda e, o=pT[ps][:, 0:ncols], i=sm[:, 0:ncols]:
                           e.activation(out=o, in_=i, func=AF.Exp), r=[smk], w=[K_("pT", ps)])

                def emit_PV(idx):
                    g, dil, nblk, rcl, b = items[idx]
                    ps = idx % NPT
                    kbi = rcl * nblk + b
                    if b == 0:
                        state["prev"] = None
                    if state["bank"] is None:
                        state["bank"] = (state["bkn"] % 2, 2 + state["bkn"] % 2)
                        state["bkn"] += 1
                        state["pend"] = []
                    bank = state["bank"]
                    qi = len(state["pend"])
                    pairs_n, pairs_d, rk = [], [], [K_("pT", ps), K_("vB", g, kbi // 4), K_("onesb")]
                    if state["prev"] is not None:
                        pps, pkbi = state["prev"]
                        pairs_n.append((vB[:, g, pkbi, :], pT[pps][:, 128:256]))
                        pairs_d.append((onesb, pT[pps][:, 128:256]))
                        rk += [K_("pT", pps), K_("vB", g, pkbi // 4)]
                    pairs_n.append((vB[:, g, kbi, :], pT[ps][:, 0:128]))
                    pairs_d.append((onesb, pT[ps][:, 0:128]))
                    mm(acc[bank[0]][:, qi * 128:(qi + 1) * 128], pairs_n, r=rk, w=[K_("acc", bank[0])])
                    mm(acc[bank[1]][:, qi * 128:(qi + 1) * 128], pairs_d, r=rk, w=[K_("acc", bank[1])])
                    state["pend"].append((rcl, b))
                    state["prev"] = (ps, kbi)
                    if len(state["pend"]) == 4:
                        r0, b0 = state["pend"][0]
                        if g == 0:
                            dn = an[:, b0 * 128:(b0 + 4) * 128]
                            dd = ad[:, b0 * 128:(b0 + 4) * 128]
                            sn = acc[bank[0]][:, 0:512]
                            sd = acc[bank[1]][:, 0:512]
                        elif g == 1:
                            dn = an[:, r0:S:4]
                            dd = ad[:, r0:S:4]
                            sn = acc[bank[0]][:, 0:512]
                            sd = acc[bank[1]][:, 0:512]
                        else:
                            dn = an.rearrange("p (i r) -> p r i", r=16)[:, r0:r0 + 4, :]
                            dd = ad.rearrange("p (i r) -> p r i", r=16)[:, r0:r0 + 4, :]
                            sn = acc[bank[0]][:, 0:512].rearrange("p (a b) -> p a b", a=4)
                            sd = acc[bank[1]][:, 0:512].rearrange("p (a b) -> p a b", a=4)
                        em.add("dve", lambda e, o=dn, i=sn: e.tensor_tensor(out=o, in0=o, in1=i, op=ALU.add),
                               r=[K_("acc", bank[0]), K_("an")], w=[K_("an")])
                        em.add("dve", lambda e, o=dd, i=sd: e.tensor_tensor(out=o, in0=o, in1=i, op=ALU.add),
                               r=[K_("acc", bank[1]), K_("ad")], w=[K_("ad")])
                        state["bank"] = None
                        state["pend"] = []

                for i_ in range(len(items) + LA):
                    if i_ < len(items):
                        emit_S(i_)
                    if i_ - LA >= 0 and p1s >= 5:
                        emit_PV(i_ - LA)
                assert state["bank"] is None or p1s < 5
                if p1s < 6:
                    continue
                em.add("dve", lambda e: e.reciprocal(out=ad, in_=ad), r=[K_("ad")], w=[K_("ad")])
                em.add("dve", lambda e, j=j: e.tensor_tensor(out=ybT[:, j, :], in0=an, in1=ad, op=ALU.mult),
                       r=[K_("an"), K_("ad")], w=[K_("ybT", j)])

        def phase1s():
            al1 = Alloc(phase_base, NBYTES)
            qTs = al1.get([12, NT], F32)
            kTs = al1.get([12, NT], F32)
            vTs = al1.get([12, NT], F32)
            knew = al1.get([12, 128], F32)
            vnew = al1.get([12, 128], F32)
            kct = [al1.get([9, 512], F32) for _ in range(2)]
            vct = [al1.get([9, 512], F32)] * 2
            kcT = al1.get([9, 4, 128], F32)
            sbias = al1.get([48], F32)
            nbias = al1.get([192], F32)
            stm = [al1.get([48], F32) for _ in range(2)]
            pTs = [al1.get([48], F32) for _ in range(2)]
            stn_ = al1.get([192], F32)
            pTn = al1.get([192], F32)
            dsb = al1.get([64], F32)
            stg1 = [al1.get([2048], F32) for _ in range(2)]
            dma("sp", sbias, c_sbias, r=[], w=[K_("sbias")], ring="io")
            dma("sp", nbias[0:16, :], c_nbias, r=[], w=[K_("nbias")], ring="io")
            xb, xk = bufA, K_("A", "all")
            dma("sp", xb[:, 0:KC, 0:NT], xTsm.rearrange("p (a b) -> p a b", a=KC), r=[K_("xTsm")], w=[xk], ring="io")
            for j in range(4):
                for kind, dst in (("Q", qTs), ("K", kTs), ("V", vTs)):
                    wt, wkey = ws.get("%s%d" % (kind, j))
                    for c in range(3):
                        a = next_acc2()
                        if not em.dry:
                            mm(acc[a][:, 0:NT], [(wt[:, kc, c * 128:(c + 1) * 128], xb[:, kc, 0:NT]) for kc in range(KC)],
                               r=[wkey, xk], w=[K_("acc", a)])
                        copy_op(dst[:, c * 4 + j, :], acc[a][:, 0:NT], r=[K_("acc", a)], w=[K_("s" + kind)])
            for kind, src, dst, which in (("K", kTs, knew, 0), ("V", vTs, vnew, 1)):
                for g4 in range(3):
                    t = next_tp()
                    for i in range(4):
                        tr(tp[t][0:NT, i * 128:(i + 1) * 128], src[:, g4 * 4 + i, :], ident, r=[K_("s" + kind), K_("ident")],
                           w=[K_("tp", t)])
                    copy_op(dst[0:NT, g4 * 4:(g4 + 1) * 4, :], tp[t][0:NT, 0:512].rearrange("p (a b) -> p a b", a=4),
                            r=[K_("tp", t)], w=[K_("new" + kind)])
                for g in range(3):
                    L = GROUPS[g][0]
                    for b in range(NSQ):
                        dstv = kv_s[g][b, L - DS:L, which * 512:(which + 1) * 512].rearrange("t (j d) -> t j d", j=4)
                        dma("sp", dstv, dst[b * DS:(b + 1) * DS, g * 4:(g + 1) * 4, :], r=[K_("new" + kind)], w=[],
                            ring="io")
            for gh in range(12):
                mm(sTb[0][0:NT, gh * 16:(gh + 1) * 16], [(kTs[:, gh, :], qTs[:, gh, :])], r=[K_("sK"), K_("sQ")],
                   w=[K_("sT", 0)])
            em.add("dve", lambda e: e.scalar_tensor_tensor(out=stn_[0:NT, :], in0=sTb[0][0:NT, 0:192], scalar=SCALE,
                                                           in1=nbias[0:NT, :], op0=ALU.mult, op1=ALU.add),
                   r=[K_("sT", 0), K_("nbias")], w=[K_("stn")])
            em.add("act", lambda e: e.activation(out=pTn[0:NT, :], in_=stn_[0:NT, :], func=AF.Exp), r=[K_("stn")],
                   w=[K_("pTn")])
            NB, DB = 2, 3
            first = [True, True]
            for h in range(4):
                for g in range(3):
                    gh = g * 4 + h
                    mm(acc[NB][:, h * 16:(h + 1) * 16], [(vnew[0:NT, gh, :], pTn[0:NT, gh * 16:(gh + 1) * 16])],
                       r=[K_("newV"), K_("pTn")], w=[K_("acc", NB)], skip=True, first_start=first[0])
                    first[0] = False
                    mm(acc[DB][:, h * 16:(h + 1) * 16], [(onesf[0:NT, :], pTn[0:NT, gh * 16:(gh + 1) * 16])],
                       r=[K_("onesf"), K_("pTn")], w=[K_("acc", DB)], skip=True, first_start=first[1])
                    first[1] = False
            units = [(0, 0)] + [(1, cl) for cl in range(DS)] + [(2, cl) for cl in range(DS)]
            for b in range(NSQ):
                bb = b % 2
                for which_ in (0, 1):
                    for ui, (g, cl) in enumerate(units):
                        L, dil = GROUPS[g]
                        rows = slice(0, 128) if g == 0 else slice(cl, cl + dil * 127 + 1, dil)
                        if which_ == 0:
                            dma("sp", kct[bb][:, ui, :], caches[g][b, rows, 0:512], r=[], w=[K_("kct", bb)], ring="io")
                        else:
                            dma("sp", vct[bb][:, ui, :], caches[g][b, rows, 512:1024], r=[], w=[K_("vct")], ring="io")
                for ui, (g, cl) in enumerate(units):
                    t = next_tp()
                    for h in range(4):
                        tr(tp[t][:, h * 128:(h + 1) * 128], kct[bb][:, ui, h * 128:(h + 1) * 128], ident,
                           r=[K_("kct", bb), K_("ident")], w=[K_("tp", t)])
                    copy_op(kcT[:, ui, :, :], tp[t][:, 0:512].rearrange("p (a b) -> p a b", a=4), r=[K_("tp", t)],
                            w=[K_("kcT", ui)])
                for ui, (g, cl) in enumerate(units):
                    ts = list(range(DS)) if g == 0 else [cl]
                    nt_ = len(ts)
                    q0 = b * DS + ts[0]
                    for h in range(4):
                        col = g * 16 + h * 4 + ts[0]
                        mm(sTb[1][:, col:col + nt_], [(kcT[:, ui, h, :], qTs[:, g * 4 + h, q0:q0 + nt_])],
                           r=[K_("kcT", ui), K_("sQ")], w=[K_("sT", 1)])
                em.add("dve", lambda e, o=stm[bb], i=sTb[1][:, 0:48]:
                       e.scalar_tensor_tensor(out=o, in0=i, scalar=SCALE, in1=sbias, op0=ALU.mult, op1=ALU.add),
                       r=[K_("sT", 1), K_("sbias")], w=[K_("stm", bb)])
                em.add("act", lambda e, o=pTs[bb], i=stm[bb]: e.activation(out=o, in_=i, func=AF.Exp),
                       r=[K_("stm", bb)], w=[K_("pTs", bb)])
                for ui, (g, cl) in enumerate(units):
                    ts = list(range(DS)) if g == 0 else [cl]
                    nt_ = len(ts)
                    q0 = b * DS + ts[0]
                    for h in range(4):
                        col = g * 16 + h * 4 + ts[0]
                        mm(acc[NB][:, h * 16 + q0: h * 16 + q0 + nt_],
                           [(vct[bb][:, ui, h * 128:(h + 1) * 128], pTs[bb][:, col:col + nt_])],
                           r=[K_("vct"), K_("pTs", bb)], w=[K_("acc", NB)], skip=True, first_start=False)
                        mm(acc[DB][:, h * 16 + q0: h * 16 + q0 + nt_],
                           [(onesf, pTs[bb][:, col:col + nt_])],
                           r=[K_("onesf"), K_("pTs", bb)], w=[K_("acc", DB)], skip=True, first_start=False)
            load_fm_table(smix.rearrange("p c s i -> p c (s i)"), st_mix, NSQ * 2, KC, stg1, K_("stab"))
            load_fm_table(sffn.rearrange("p c s i -> p c (s i)"), st_ffn, NSQ * 2, FC, stg1, K_("stab"))
            copy_op(dsb, acc[DB][:, 0:64], r=[K_("acc", DB)], w=[K_("dsb")], eng="act")
            em.add("dve", lambda e: e.reciprocal(out=dsb, in_=dsb), r=[K_("dsb")], w=[K_("dsb")])
            em.add("dve", lambda e: e.tensor_tensor(out=ybTs.rearrange("p a b -> p (a b)"), in0=acc[NB][:, 0:64], in1=dsb,
                                                    op=ALU.mult),
                   r=[K_("acc", NB), K_("dsb")], w=[K_("ybTs")])

        def phase2_setup():
            al2 = Alloc(phase_base, NBYTES)
            P = {}
            P["xv"] = al2.get([4, D], F32)
            P["M"] = al2.get([16, 512], BF16)
            P["tA"] = al2.get([4, 514], F32)
            P["tB"] = al2.get([4, 512], F32)
            P["tC"] = al2.get([4, 512], F32)
            P["lng"] = al2.get([D], F32)
            P["lnb"] = al2.get([D], F32)
            P["st"] = al2.get([16, 6], F32)
            P["mv"] = al2.get([4, 8], F32)
            P["s_xT"] = al2.get([16, NT], BF16)
            P["s_ya"] = al2.get([16, NT], BF16)
            P["s_M"] = al2.get([16, NT], BF16)
            P["s_h0"] = al2.get([cfg.FG, NT], BF16)
            P["s_h1"] = al2.get([cfg.FG, NT], BF16)
            P["s_tA"] = al2.get([4, NSQ * (DS + 2)], F32)
            P["s_tB"] = al2.get([4, NT], F32)
            P["s_tC"] = al2.get([4, NT], F32)
            P["s_xv"] = al2.get([1, D], F32)
            P["s_st"] = al2.get([4, 6], F32)
            P["s_mv"] = al2.get([1, 8], F32)
            return P

        class Ctx:
            pass

        def make_ctx(P, prompt, sq=0, tt=0):
            cx = Ctx()
            cx.prompt = prompt
            if prompt:
                cx.N, cx.nseq, cx.L, cx.NS, cx.RP, cx.pre = 512, 1, 512, 4, 128, "p"
                cx.xT, cx.yaT, cx.M = bufA, bufB, P["M"]
                cx.tA, cx.tB, cx.tC, cx.xv = P["tA"], P["tB"], P["tC"], P["xv"]
                cx.hT = [(bufB[:, 0:8, :], "B", 0), (bufB[:, 8:16, :], "B", 8)]
                cx.kx, cx.kya, cx.kM = "A", "B", "M"
                cx.st, cx.mv = P["st"], P["mv"]
                cx.sq, cx.tt = sq, tt
                cx.first_tile, cx.last_tile = tt == 0, tt == 3
                cx.row0 = sq * S + tt * 512
            else:
                cx.N, cx.nseq, cx.L, cx.NS, cx.RP, cx.pre = NT, NSQ, DS, 1, NT, "s"
                cx.xT, cx.yaT, cx.M = P["s_xT"], P["s_ya"], P["s_M"]
                cx.tA, cx.tB, cx.tC, cx.xv = P["s_tA"], P["s_tB"], P["s_tC"], P["s_xv"]
                cx.hT = [(P["s_h0"], "sH0", 0), (P["s_h1"], "sH1", 0)]
                cx.kx, cx.kya, cx.kM = "sA", "sB", "sM"
                cx.st, cx.mv = P["s_st"], P["s_mv"]
                cx.first_tile = cx.last_tile = False
            return cx

        def phase2(P, ctxs, nxt=None):
            def k_(cx, name, *idx):
                return (cx.pre + name,) + idx

            def v3(cx, ap):
                return ap.rearrange("p (s l) -> p s l", s=cx.nseq)

            def halo_view(cx, j):
                return cx.tA[:, j, 0:cx.nseq * (cx.L + 2)].rearrange("p (s l) -> p s l", s=cx.nseq)

            def ln_load(which):
                gg, bb = ((ln1g, ln1b), (ln2g, ln2b))[which]
                dma("sp", P["lng"], gg.partition_broadcast(128), r=[], w=[K_("lng")], ring="io")
                dma("sp", P["lnb"], bb.partition_broadcast(128), r=[], w=[K_("lnb")], ring="io")

            def load_xT(sq_, tt_):
                dma("sp", bufA[:, 0:KC, :], xTs[sq_ * 4 + tt_][:, 0:KC * 512].rearrange("p (a b) -> p a b", a=KC),
                    r=[K_("xTs", sq_ * 4 + tt_)], w=[K_("A", c) for c in range(16)], ring="io")
                P["xT_for"] = (sq_, tt_)

            for cx in ctxs:
                if cx.prompt:
                    if P.get("xT_for") != (cx.sq, cx.tt):
                        load_xT(cx.sq, cx.tt)
                    cx.ybv = (lambda j, tt=cx.tt: ybT[:, j, tt * 512:(tt + 1) * 512])
                    cx.ybk = [K_("ybT", j) for j in range(4)]
                else:
                    dma("sp", cx.xT[:, 0:KC, :], xTsm.rearrange("p (a b) -> p a b", a=KC), r=[K_("xTsm")],
                        w=[K_(cx.kx, c) for c in range(16)], ring="io")
                    cx.ybv = (lambda j: ybTs[:, j, :])
                    cx.ybk = [K_("ybTs")]
                cx.xkeys = [K_(cx.kx, c) for c in range(KC)]
                cx.mkeys = [K_(cx.kM, c) for c in range(KC)]

            nchk = D // 512

            def ln_stats(cx, s, c):
                RP = cx.RP
                em.add("dve", lambda e: e.bn_stats(out=cx.st[0:RP, s * nchk + c, :], in_=cx.xv[0:RP, s, c * 512:(c + 1) * 512]),
                       r=[k_(cx, "xv", s, c)], w=[k_(cx, "st", s, c)])

            def ln_finish(cx):
                RP, NS = cx.RP, cx.NS
                st, mv = cx.st, cx.mv
                mk = k_(cx, "mv")
                for s in range(NS):
                    em.add("dve", lambda e, s=s: e.bn_aggr(out=mv[0:RP, s, 0:2],
                                                          in_=st[0:RP, s * nchk:(s + 1) * nchk, :].rearrange("p a b -> p (a b)")),
                           r=[k_(cx, "st", s, c) for c in range(nchk)], w=[mk])
                em.add("dve", lambda e: e.tensor_scalar_add(out=mv[0:RP, 0:NS, 2], in0=mv[0:RP, 0:NS, 1], scalar1=EPS),
                       r=[mk], w=[mk])
                em.add("act", lambda e: e.activation(out=mv[0:RP, 0:NS, 2], in_=mv[0:RP, 0:NS, 2], func=AF.Sqrt),
                       r=[mk], w=[mk])
                em.add("dve", lambda e: e.reciprocal(out=mv[0:RP, 0:NS, 2], in_=mv[0:RP, 0:NS, 2]), r=[mk], w=[mk])
                for s in range(NS):
                    for q in range(nchk):
                        xq = cx.xv[0:RP, s, q * 512:(q + 1) * 512]
                        xk = k_(cx, "xv", s, q)
                        em.add("dve", lambda e, xq=xq, s=s, q=q: e.scalar_tensor_tensor(
                            out=xq, in0=xq, scalar=mv[0:RP, s, 0:1], in1=P["lng"][0:RP, q * 512:(q + 1) * 512],
                            op0=ALU.subtract, op1=ALU.mult), r=[xk, mk, K_("lng")], w=[xk])
                        em.add("dve", lambda e, xq=xq, s=s, q=q: e.scalar_tensor_tensor(
                            out=xq, in0=xq, scalar=mv[0:RP, s, 2:3], in1=P["lnb"][0:RP, q * 512:(q + 1) * 512],
                            op0=ALU.mult, op1=ALU.add), r=[xk, mk, K_("lnb")], w=[xk])
                        yield s, q

            def wsblock(name, stage):
                wt, wkey = ws.get(name)
                kcn = blocks[name][1]
                for j4 in range(4):
                    for cx in ctxs:
                        rhs_of, rkeys, evac = stage(cx)
                        a = next_acc()
                        if not em.dry:
                            mm(acc[a][:, 0:cx.N], [(wt[:, kc, j4 * 128:(j4 + 1) * 128], rhs_of(kc)) for kc in range(kcn)],
                               r=[wkey] + rkeys, w=[K_("acc", a)])
                        evac(j4, acc[a][:, 0:cx.N], K_("acc", a))

            def conv3(cx, j, wtab, c):
                hv = halo_view(cx, j)
                L, N = cx.L, cx.N
                o = v3(cx, cx.tB[:, j, 0:N])
                srck, hk_, dstk = k_(cx, "tA", j), k_(cx, "tAh", j), k_(cx, "tB", j)
                em.add("act", lambda e: e.activation(out=o, in_=hv[:, :, 0:L], func=AF.Identity, scale=wtab[:, c, 0:1]),
                       r=[srck, hk_], w=[dstk])
                em.add("dve", lambda e: e.scalar_tensor_tensor(out=o, in0=hv[:, :, 1:L + 1], scalar=wtab[:, c, 1:2], in1=o,
                                                               op0=ALU.mult, op1=ALU.add), r=[srck, hk_, dstk], w=[dstk])
                em.add("dve", lambda e: e.scalar_tensor_tensor(out=o, in0=hv[:, :, 2:L + 2], scalar=wtab[:, c, 2:3], in1=o,
                                                               op0=ALU.mult, op1=ALU.add), r=[srck, dstk], w=[dstk])

            def set_halo(cx, j, c, htab, hkey, st_src):
                hv = halo_view(cx, j)
                if cx.prompt:
                    if cx.first_tile:
                        em.add("pool", lambda e: e.memset(hv[:, :, 0:2], 0.0), r=[], w=[k_(cx, "tAh", j)])
                    else:
                        em.add("pool", lambda e: e.tensor_copy(out=hv[:, 0, 0:2], in_=htab[:, c, :]), r=[K_(hkey, c)],
                               w=[k_(cx, "tAh", j)])
                else:
                    stab = smix if st_src is st_mix else sffn
                    em.add("pool", lambda e: e.tensor_copy(out=hv[:, :, 0:2], in_=stab[:, c, :, :]), r=[K_("stab")],
                           w=[k_(cx, "tAh", j)])

            def save_halo(cx, j, c, htab, hkey, out_p, out_s):
                hv = halo_view(cx, j)
                L = cx.L
                if cx.prompt:
                    em.add("pool", lambda e: e.tensor_copy(out=htab[:, c, :], in_=hv[:, 0, L:L + 2]),
                           r=[k_(cx, "tA", j)], w=[K_(hkey, c)])
                else:
                    stab = smix if out_s is cm_s else sffn
                    em.add("pool", lambda e: e.tensor_copy(out=stab[:, c, :, :], in_=hv[:, :, L:L + 2]),
                           r=[k_(cx, "tA", j), k_(cx, "tAh", j)], w=[K_("stab")])

            for i in range(NIB):
                def st_h(cx):
                    def ev(j4, a_ap, a_key):
                        copy_op(cx.tB[:, j4, 0:cx.N], a_ap, r=[a_key], w=[k_(cx, "tB", j4)], eng="act")
                    return (lambda kc: cx.xT[:, kc, 0:cx.N]), cx.xkeys, ev
                wsblock("H%d" % i, st_h)

                def st_c(cx, i=i):
                    def ev(j4, a_ap, a_key):
                        c = 4 * i + j4
                        set_halo(cx, j4, c, hu, "hu", st_mix)
                        hv = halo_view(cx, j4)
                        L, N = cx.L, cx.N
                        em.add("dve", lambda e: e.tensor_tensor(out=hv[:, :, 2:L + 2], in0=v3(cx, a_ap),
                                                                in1=v3(cx, cx.tB[:, j4, 0:N]), op=ALU.mult),
                               r=[a_key, k_(cx, "tB", j4)], w=[k_(cx, "tA", j4)])
                        save_halo(cx, j4, c, hu, "hu", cm_p, cm_s)
                        conv3(cx, j4, wcm, c)
                    return (lambda kc: cx.xT[:, kc, 0:cx.N]), cx.xkeys, ev
                wsblock("C%d" % i, st_c)
                if i == 0:
                    for f_ in P.pop("deferred", []):
                        f_()
                    for cx in ctxs:
                        for s in range(cx.NS):
                            if cx.prompt:
                                dma("sp", cx.xv[:, s, :], x_p[cx.row0 + s * 128: cx.row0 + (s + 1) * 128, :], r=[],
                                    w=[k_(cx, "xv", s, q) for q in range(NIB)], ring="io")
                            else:
                                dma("sp", cx.xv[0:NT, 0, :], x_s, r=[], w=[k_(cx, "xv", 0, q) for q in range(NIB)], ring="io")

                def st_b(cx, i=i):
                    def ev(j4, a_ap, a_key):
                        c = 4 * i + j4
                        N = cx.N
                        em.add("dve", lambda e: e.tensor_tensor(out=cx.yaT[:, c, 0:N], in0=a_ap, in1=cx.tB[:, j4, 0:N],
                                                                op=ALU.mult),
                               r=[a_key, k_(cx, "tB", j4)], w=[K_(cx.kya, c)])
                    return (lambda kc: cx.xT[:, kc, 0:cx.N]), cx.xkeys, ev
                wsblock("B%d" % i, st_b)
            ln_load(0)
            for i in range(NIB):
                def st_g(cx):
                    def ev(j4, a_ap, a_key):
                        N = cx.N
                        em.add("act", lambda e: e.activation(out=cx.tB[:, j4, 0:N], in_=a_ap, func=AF.Sigmoid), r=[a_key],
                               w=[k_(cx, "tB", j4)])
                    return (lambda kc: cx.xT[:, kc, 0:cx.N]), cx.xkeys, ev
                wsblock("GA%d" % i, st_g)

                def st_wa(cx):
                    def ev(j4, a_ap, a_key):
                        N = cx.N
                        em.add("dve", lambda e: e.tensor_tensor(out=cx.tC[:, j4, 0:N], in0=a_ap, in1=cx.tB[:, j4, 0:N],
                                                                op=ALU.mult),
                               r=[a_key, k_(cx, "tB", j4)], w=[k_(cx, "tC", j4)])
                    return (lambda kc: cx.yaT[:, kc, 0:cx.N]), [K_(cx.kya, c) for c in range(KC)], ev
                wsblock("WA%d" % i, st_wa)
                wsblock("GB%d" % i, st_g)

                def st_wb(cx, i=i):
                    def ev(j4, a_ap, a_key):
                        c = 4 * i + j4
                        N = cx.N
                        em.add("dve", lambda e: e.tensor_tensor(out=cx.tB[:, j4, 0:N], in0=a_ap, in1=cx.tB[:, j4, 0:N],
                                                                op=ALU.mult),
                               r=[a_key, k_(cx, "tB", j4)], w=[k_(cx, "tB", j4)])
                        em.add("pool", lambda e: e.tensor_tensor(out=cx.M[:, c, 0:N], in0=cx.tB[:, j4, 0:N],
                                                                 in1=cx.tC[:, j4, 0:N], op=ALU.add),
                               r=[k_(cx, "tB", j4), k_(cx, "tC", j4)], w=[K_(cx.kM, c)])
                    return (lambda kc: cx.ybv(kc)), cx.ybk, ev
                wsblock("WB%d" % i, st_wb)
            for n in range(NIB):
                wt, wkey = ws.get("WO%d" % n)
                for cx in ctxs:
                    RP = cx.RP
                    for s in range(cx.NS):
                        a = next_acc()
                        if not em.dry:
                            mm(acc[a][0:RP, :], [(cx.M[:, kc, s * 128: s * 128 + RP], wt[:, kc, :]) for kc in range(KC)],
                               r=[wkey] + cx.mkeys, w=[K_("acc", a)])
                        xs_ = cx.xv[0:RP, s, n * 512:(n + 1) * 512]
                        em.add("dve", lambda e, o=xs_, i=acc[a][0:RP, :]: e.scalar_tensor_tensor(
                            out=o, in0=o, scalar=ALPHA, in1=i, op0=ALU.mult, op1=ALU.add),
                            r=[K_("acc", a), k_(cx, "xv", s, n)], w=[k_(cx, "xv", s, n)])
                for cx in ctxs:
                    for s in range(cx.NS):
                        ln_stats(cx, s, n)

            G4 = min(4, KC)
            for cx in ctxs:
                RP = cx.RP
                for s, q in ln_finish(cx):
                    k4 = q * 4
                    t = next_tp()
                    for j in range(G4):
                        tr(tp[t][:, j * RP:(j + 1) * RP], cx.xv[0:RP, s, (k4 + j) * 128:(k4 + j + 1) * 128],
                           ident[0:RP, 0:RP], r=[k_(cx, "xv", s, q), K_("ident")], w=[K_("tp", t)])
                    copy_op(cx.M[:, k4:k4 + G4, s * 128: s * 128 + RP],
                            tp[t][:, 0:G4 * RP].rearrange("p (a b) -> p a b", a=G4),
                            r=[K_("tp", t)], w=[K_(cx.kM, c) for c in range(k4, k4 + G4)])
            if nxt is not None:
                load_xT(*nxt)
            for gi, (c0, ng) in enumerate(fgroups):
                if gi == 1 or len(fgroups) == 1:
                    ln_load(1)
                for i in range(c0 // 4, (c0 + ng) // 4):
                    def st_a(cx, i=i):
                        def ev(j4, a_ap, a_key):
                            c = 4 * i + j4
                            set_halo(cx, j4, c, ha, "ha", st_ffn)
                            hv = halo_view(cx, j4)
                            L, N = cx.L, cx.N
                            em.add("act", lambda e: e.activation(out=hv[:, :, 2:L + 2], in_=v3(cx, a_ap), func=AF.Copy),
                                   r=[a_key], w=[k_(cx, "tA", j4)])
                            save_halo(cx, j4, c, ha, "ha", cf_p, cf_s)
                            conv3(cx, j4, wcf, c)
                            em.add("act", lambda e: e.activation(out=cx.tB[:, j4, 0:N], in_=cx.tB[:, j4, 0:N], func=AF.Silu),
                                   r=[k_(cx, "tB", j4)], w=[k_(cx, "tB", j4)])
                        return (lambda kc: cx.M[:, kc, 0:cx.N]), cx.mkeys, ev
                    wsblock("UA%d" % i, st_a)

                    def st_bb(cx, i=i, gi=gi, c0=c0):
                        hT, hk, ko = cx.hT[gi % 2]

                        def ev(j4, a_ap, a_key):
                            c = 4 * i + j4
                            N = cx.N
                            o_ = hT[:, c - c0, 0:N]
                            em.add("dve", lambda e: e.tensor_tensor(out=o_, in0=a_ap, in1=cx.tB[:, j4, 0:N], op=ALU.mult),
                                   r=[a_key, k_(cx, "tB", j4)], w=[K_(hk, ko + c - c0)])
                        return (lambda kc: cx.M[:, kc, 0:cx.N]), cx.mkeys, ev
                    wsblock("UB%d" % i, st_bb)
                for n in range(NIB):
                    wt, wkey = ws.get("WD%d_%d" % (gi, n))
                    for cx in ctxs:
                        RP = cx.RP
                        hT, hk, ko = cx.hT[gi % 2]
                        hkeys = [K_(hk, ko + c) for c in range(ng)]
                        for s in range(cx.NS):
                            a = next_acc()
                            if not em.dry:
                                mm(acc[a][0:RP, :], [(hT[:, c, s * 128: s * 128 + RP], wt[:, c, :]) for c in range(ng)],
                                   r=[wkey] + hkeys, w=[K_("acc", a)])
                            xs_ = cx.xv[0:RP, s, n * 512:(n + 1) * 512]
                            xk = k_(cx, "xv", s, n)
                            if gi == 0:
                                em.add("dve", lambda e, o=xs_, i=acc[a][0:RP, :]: e.scalar_tensor_tensor(
                                    out=o, in0=o, scalar=ALPHA, in1=i, op0=ALU.mult, op1=ALU.add),
                                    r=[K_("acc", a), xk], w=[xk])
                            else:
                                em.add("dve", lambda e, o=xs_, i=acc[a][0:RP, :]: e.tensor_tensor(out=o, in0=o, in1=i,
                                                                                                  op=ALU.add),
                                       r=[K_("acc", a), xk], w=[xk])
                    if gi == len(fgroups) - 1:
                        for cx in ctxs:
                            for s in range(cx.NS):
                                ln_stats(cx, s, n)
            for cx in ctxs:
                for s, q in ln_finish(cx):
                    if q == nchk - 1:
                        rk_ = [k_(cx, "xv", s, qq) for qq in range(nchk)]
                        if cx.prompt:
                            def ystore(cx=cx, s=s, rk_=rk_):
                                dma("sp", y_p[cx.row0 + s * 128: cx.row0 + (s + 1) * 128, :], cx.xv[:, s, :], r=rk_, w=[],
                                    ring="io")
                            if cx.last_tile:
                                ystore()
                            else:
                                P.setdefault("deferred", []).append(ystore)
                        else:
                            dma("sp", y_s, cx.xv[0:NT, 0, :], r=rk_, w=[], ring="io")
            for cx in ctxs:
                if cx.prompt and cx.last_tile:
                    pk = [("ptC", j) for j in range(4)]
                    store_fm_table(hu, cm_p[cx.sq], 2, KC, lambda si: P["tC"][:, si, :], pk, lambda c: K_("hu", c))
                    store_fm_table(ha, cf_p[cx.sq], 2, FC, lambda si: P["tC"][:, si, :], pk, lambda c: K_("ha", c))
            for cx in ctxs:
                if not cx.prompt:
                    pk = [("ptC", j) for j in range(4)]
                    store_fm_table(smix.rearrange("p c s i -> p c (s i)"), cm_s.rearrange("s i d -> (s i) d"), NSQ * 2, KC,
                                   lambda si: P["tC"][:, si, :], pk, K_("stab"))
                    store_fm_table(sffn.rearrange("p c s i -> p c (s i)"), cf_s.rearrange("s i d -> (s i) d"), NSQ * 2, FC,
                                   lambda si: P["tC"][:, si, :], pk, K_("stab"))


        def schedule():
            stop = getattr(cfg, "stop", 99)
            prologue()
            if stop < 1:
                return
            phase0(0)
            prologue2()
            em.barrier()
            if stop < 2:
                return
            for sq in range(NP):
                if sq == NP - 1:
                    kv_shift()
                phase1(sq)
                em.barrier()
                if cfg.sample and sq == 0:
                    phase1s()
                    em.barrier()
                if stop < 3:
                    return
                P = phase2_setup()
                for tt in range(4):
                    ctxs = [make_ctx(P, True, sq, tt)]
                    if cfg.sample and sq == 0 and tt == 1:
                        ctxs.append(make_ctx(P, False))
                    phase2(P, ctxs, nxt=(sq, tt + 1) if tt < 3 else None)
                em.barrier()

        def fix_keys():
            pass

        em.dry = True
        ws.rec = True
        schedule()
        em.dry = False
        ws.rec = False
        accn[0] = 0
        tpn[0] = 0
        evn[0] = 0
        schedule()
        assert ws.i == len(ws.seq), (ws.i, len(ws.seq))
        print("ops:", len(em.ops), {e: sum(1 for o in em.ops if o.eng == e) for e in em.ENGS})
        em.plan()
        em.emit(nc, es)
    return nc


_CACHE = {}


def _run(cfg, inputs, ncores):
    key = (cfg.D, cfg.DF, cfg.NP, cfg.NSQ, cfg.FG, cfg.sample, getattr(cfg, "stop", 99), getattr(cfg, "p1stop", 99))
    if key not in _CACHE:
        _CACHE[key] = build_program(cfg)
    nc = _CACHE[key]
    consts = make_consts()
    f = lambda a: np.ascontiguousarray(np.asarray(a, dtype=np.float32))
    shared = {
        "w_in": f(inputs["w_in"]), "w_conv_mix": f(inputs["w_conv_mix"]), "w_branch_a": f(inputs["w_branch_a"]),
        "w_branch_b": f(inputs["w_branch_b"]), "w_out": f(inputs["w_out"]),
        "ln1_g": f(inputs["ln1_g"]).reshape(1, -1), "ln1_b": f(inputs["ln1_b"]).reshape(1, -1),
        "w_up": f(inputs["w_up"]), "w_conv_ffn": f(inputs["w_conv_ffn"]), "w_down": f(inputs["w_down"]),
        "ln2_g": f(inputs["ln2_g"]).reshape(1, -1), "ln2_b": f(inputs["ln2_b"]).reshape(1, -1),
    }
    shared.update(consts)
    NP, NSQ, D, DF = cfg.NP, cfg.NSQ, cfg.D, cfg.DF
    in_maps = []
    for c in range(ncores):
        m = dict(shared)
        m["x_p"] = f(inputs["x_prompt"][c * NP:(c + 1) * NP]).reshape(NP * S, D)
        m["x_s"] = f(inputs["x_sample"][c * NSQ:(c + 1) * NSQ]).reshape(NSQ * DS, D)
        m["st_mix"] = f(inputs["state_conv_mix"][c * NSQ:(c + 1) * NSQ]).reshape(NSQ * 2, D)
        m["st_ffn"] = f(inputs["state_conv_ffn"][c * NSQ:(c + 1) * NSQ]).reshape(NSQ * 2, DF)
        for g, cname in enumerate(("cache_kv0", "cache_kv1", "cache_kv2")):
            m["c%d" % g] = f(inputs[cname][c * NSQ:(c + 1) * NSQ]).reshape(NSQ, GROUPS[g][0], 1024)
        in_maps.append(m)
    res = run_bass_kernel_spmd(nc, in_maps, core_ids=list(range(ncores)))
    R = res.results
    cat = lambda name: np.concatenate([np.asarray(r[name], dtype=np.float32) for r in R], axis=0)
    B = NP * ncores
    BS = NSQ * ncores
    outs = (
        cat("y_p").reshape(B, S, D),
        cat("y_s").reshape(BS, DS, D),
        cat("cm_p").reshape(B, 2, D),
        cat("kv0_p").reshape(B, 128, 2, 4, 128),
        cat("kv1_p").reshape(B, 512, 2, 4, 128),
        cat("kv2_p").reshape(B, 2048, 2, 4, 128),
        cat("cf_p").reshape(B, 2, DF),
        cat("cm_s").reshape(BS, 2, D),
        cat("kv0_s").reshape(BS, 128, 2, 4, 128),
        cat("kv1_s").reshape(BS, 512, 2, 4, 128),
        cat("kv2_s").reshape(BS, 2048, 2, 4, 128),
        cat("cf_s").reshape(BS, 2, DF),
    )
    return outs


def kernel(**inputs):
    cfg = Cfg()
    return _run(cfg, inputs, NCORES)
```
